# Optimizing a Trainium2 kernel written in Bass

```python
import jax, jax.numpy as jnp
from jax import lax
import numpy as np

D_MODEL = 2048
BATCH = 2
SEQ = 4096
DEPTH = 2
DEC_BATCH = 32
DEC_SEQ = 64
PAST_LEN = 2048

CHUNK = 64
N_MIXERS = 2
N_CONV_LAYERS = (DEPTH + 1) // 2
N_ATTN_LAYERS = DEPTH // 2
CONV_WIDTH = 3
N_HEADS = 16
N_KV_HEADS = 4
HEAD_DIM = 128
N_IDX_HEADS = 16
IDX_DIM = 128
INDEX_TOPK = 256
QUERY_BLOCK = 128
D_FF = 5632
ROPE_THETA = 10000.0
LN_EPS = 1e-5
DEEPNORM_ALPHA = (2 * DEPTH) ** 0.25
DEEPNORM_BETA = (8 * DEPTH) ** -0.25
ATTN_SCALE = HEAD_DIM ** -0.5
IDX_W_SCALE = (N_IDX_HEADS ** -0.5) * (IDX_DIM ** -0.5)
Q_WIDTH = N_HEADS * HEAD_DIM
KV_WIDTH = N_KV_HEADS * HEAD_DIM
IQ_WIDTH = N_IDX_HEADS * IDX_DIM
ATTN_SPLITS = (Q_WIDTH, Q_WIDTH + KV_WIDTH, Q_WIDTH + 2 * KV_WIDTH, Q_WIDTH + 2 * KV_WIDTH + IQ_WIDTH, Q_WIDTH + 2 * KV_WIDTH + IQ_WIDTH + IDX_DIM)
ATTN_IN_WIDTH = ATTN_SPLITS[-1] + N_IDX_HEADS

kernel_name = 'hybrid_shortconv_dsa_stream_step'


def _layer_norm(x, g, b):
    xf = x.astype(jnp.float32)
    mu = jnp.mean(xf, axis=-1, keepdims=True)
    xc = xf - mu
    var = jnp.mean(xc * xc, axis=-1, keepdims=True)
    return (xc * lax.rsqrt(var + LN_EPS) * g.astype(jnp.float32) + b.astype(jnp.float32)).astype(x.dtype)


def _rope(x, pos):
    half = x.shape[-1] // 2
    inv_freq = ROPE_THETA ** (-jnp.arange(half, dtype=jnp.float32) / half)
    ang = pos.astype(jnp.float32)[:, None] * inv_freq[None, :]
    cos = jnp.cos(ang)[:, None, :].astype(x.dtype)
    sin = jnp.sin(ang)[:, None, :].astype(x.dtype)
    x1, x2 = x[..., :half], x[..., half:]
    return jnp.concatenate([x1 * cos - x2 * sin, x1 * sin + x2 * cos], axis=-1)


def _causal_dwconv3(xp, w):
    t = xp.shape[1] - (CONV_WIDTH - 1)
    return w[0] * xp[:, 0:t] + w[1] * xp[:, 1:t + 1] + w[2] * xp[:, 2:t + 2]


def _short_conv_mixer(x, prev, w_in, conv_w, w_out):
    gate_b, gate_c, val = jnp.split(x @ w_in, 3, axis=-1)
    u = jnp.concatenate([prev, gate_c * val], axis=1)
    y = (gate_b * _causal_dwconv3(u, conv_w)) @ w_out
    return y, u[:, -(CONV_WIDTH - 1):]


def _conv_glu(x, prev, w_in, conv_w, conv_b, w_down):
    g, up = jnp.split(x @ w_in, 2, axis=-1)
    gp = jnp.concatenate([prev, g], axis=1)
    h = jax.nn.silu(_causal_dwconv3(gp, conv_w) + conv_b) * up
    return h @ w_down, gp[:, -(CONV_WIDTH - 1):]


def _attn_project(x, pos, w_in, kn_g, kn_b):
    b, t = x.shape[:2]
    q, k, v, iq, ik, iw = jnp.split(x @ w_in, list(ATTN_SPLITS), axis=-1)
    q = _rope(q.reshape(b, t, N_HEADS, HEAD_DIM), pos)
    k = _rope(k.reshape(b, t, N_KV_HEADS, HEAD_DIM), pos)
    v = v.reshape(b, t, N_KV_HEADS, HEAD_DIM)
    iq = _rope(iq.reshape(b, t, N_IDX_HEADS, IDX_DIM), pos)
    ik = _rope(_layer_norm(ik, kn_g, kn_b)[:, :, None, :], pos)[:, :, 0]
    return q, k, v, iq, ik, iw * IDX_W_SCALE


def _dsa_attend(q, iq, iw, k_all, v_all, ik_all, q_pos, k_pos, topk):
    tq = q.shape[0]
    qk_idx = jnp.einsum('qhd,sd->qhs', iq, ik_all).astype(jnp.float32)
    index = jnp.einsum('qhs,qh->qs', jax.nn.relu(qk_idx), iw.astype(jnp.float32))
    admissible = (k_pos[None, :] // CHUNK) <= (q_pos[:, None] // CHUNK)
    index = jnp.where(admissible, index, -jnp.inf)
    _, sel = lax.top_k(index, topk)
    valid = jnp.take_along_axis(admissible, sel, axis=1)
    k_sel = k_all[sel]
    v_sel = v_all[sel]
    qg = q.reshape(tq, N_KV_HEADS, N_HEADS // N_KV_HEADS, HEAD_DIM)
    s = jnp.einsum('qgrd,qkgd->qgrk', qg, k_sel).astype(jnp.float32) * ATTN_SCALE
    s = jnp.where(valid[:, None, None, :], s, -jnp.inf)
    p = jax.nn.softmax(s, axis=-1).astype(v_all.dtype)
    o = jnp.einsum('qgrk,qkgd->qgrd', p, v_sel)
    return o.reshape(tq, N_HEADS * HEAD_DIM)


def _dsa_prompt(q, iq, iw, k, v, ik, topk):
    t = q.shape[1]
    nb = t // QUERY_BLOCK
    pos = jnp.arange(t, dtype=jnp.int32)

    def per_seq(q1, iq1, iw1, k1, v1, ik1):
        def blk(a):
            qb, iqb, iwb, pb = a
            return _dsa_attend(qb, iqb, iwb, k1, v1, ik1, pb, pos, topk)
        out = lax.map(blk, (q1.reshape(nb, QUERY_BLOCK, N_HEADS, HEAD_DIM),
                            iq1.reshape(nb, QUERY_BLOCK, N_IDX_HEADS, IDX_DIM),
                            iw1.reshape(nb, QUERY_BLOCK, N_IDX_HEADS),
                            pos.reshape(nb, QUERY_BLOCK)))
        return out.reshape(t, N_HEADS * HEAD_DIM)

    return jax.vmap(per_seq)(q, iq, iw, k, v, ik)


def _dsa_sample(q, iq, iw, k_all, v_all, ik_all, q_pos, topk):
    k_pos = jnp.arange(k_all.shape[1], dtype=jnp.int32)
    return lax.map(lambda a: _dsa_attend(a[0], a[1], a[2], a[3], a[4], a[5], q_pos, k_pos, topk),
                   (q, iq, iw, k_all, v_all, ik_all))


def setup_inputs(seed: int = 0) -> dict:
    key = jax.random.key(seed)
    ks = jax.random.split(key, 24)
    f32 = jnp.float32

    def nrm(k, shape, scale):
        return jax.random.normal(k, shape, f32) * scale

    d_scale = D_MODEL ** -0.5
    mix_col = jnp.concatenate([jnp.ones((2 * D_MODEL,), f32), jnp.full((D_MODEL,), DEEPNORM_BETA, f32)])
    attn_col = jnp.ones((ATTN_IN_WIDTH,), f32).at[ATTN_SPLITS[1]:ATTN_SPLITS[2]].set(DEEPNORM_BETA)
    return {
        'x_prompt': nrm(ks[0], (BATCH, SEQ, D_MODEL), 1.0),
        'x_sample': nrm(ks[1], (DEC_BATCH, DEC_SEQ, D_MODEL), 1.0),
        'state_conv_mix': nrm(ks[2], (N_CONV_LAYERS, DEC_BATCH, CONV_WIDTH - 1, D_MODEL), 0.5),
        'cache_k': nrm(ks[3], (N_ATTN_LAYERS, DEC_BATCH, PAST_LEN, N_KV_HEADS, HEAD_DIM), 1.0),
        'cache_v': nrm(ks[4], (N_ATTN_LAYERS, DEC_BATCH, PAST_LEN, N_KV_HEADS, HEAD_DIM), 0.5),
        'cache_idx_k': nrm(ks[5], (N_ATTN_LAYERS, DEC_BATCH, PAST_LEN, IDX_DIM), 1.0),
        'state_ffn_conv': nrm(ks[6], (DEPTH, DEC_BATCH, CONV_WIDTH - 1, D_FF), 0.5),
        'mix_w_in': nrm(ks[7], (N_CONV_LAYERS, D_MODEL, 3 * D_MODEL), d_scale) * mix_col,
        'mix_conv_w': nrm(ks[8], (N_CONV_LAYERS, CONV_WIDTH, D_MODEL), CONV_WIDTH ** -0.5),
        'mix_w_out': nrm(ks[9], (N_CONV_LAYERS, D_MODEL, D_MODEL), d_scale * DEEPNORM_BETA),
        'attn_w_in': nrm(ks[10], (N_ATTN_LAYERS, D_MODEL, ATTN_IN_WIDTH), d_scale) * attn_col,
        'idx_k_norm_g': 1.0 + nrm(ks[11], (N_ATTN_LAYERS, IDX_DIM), 0.01),
        'idx_k_norm_b': nrm(ks[12], (N_ATTN_LAYERS, IDX_DIM), 0.01),
        'attn_w_out': nrm(ks[13], (N_ATTN_LAYERS, Q_WIDTH, D_MODEL), (Q_WIDTH ** -0.5) * DEEPNORM_BETA),
        'ffn_w_in': nrm(ks[14], (DEPTH, D_MODEL, 2 * D_FF), d_scale * DEEPNORM_BETA),
        'ffn_conv_w': nrm(ks[15], (DEPTH, CONV_WIDTH, D_FF), CONV_WIDTH ** -0.5),
        'ffn_conv_b': nrm(ks[16], (DEPTH, D_FF), 0.01),
        'ffn_w_down': nrm(ks[17], (DEPTH, D_FF, D_MODEL), (D_FF ** -0.5) * DEEPNORM_BETA),
        'ln1_g': 1.0 + nrm(ks[18], (DEPTH, D_MODEL), 0.01),
        'ln1_b': nrm(ks[19], (DEPTH, D_MODEL), 0.01),
        'ln2_g': 1.0 + nrm(ks[20], (DEPTH, D_MODEL), 0.01),
        'ln2_b': nrm(ks[21], (DEPTH, D_MODEL), 0.01),
    }


def reference(x_prompt, x_sample, state_conv_mix, cache_k, cache_v, cache_idx_k, state_ffn_conv,
              mix_w_in, mix_conv_w, mix_w_out, attn_w_in, idx_k_norm_g, idx_k_norm_b, attn_w_out,
              ffn_w_in, ffn_conv_w, ffn_conv_b, ffn_w_down, ln1_g, ln1_b, ln2_g, ln2_b):
    bp, tp = x_prompt.shape[:2]
    ts = x_sample.shape[1]
    past = cache_k.shape[2]
    pos_p = jnp.arange(tp, dtype=jnp.int32)
    pos_s = past + jnp.arange(ts, dtype=jnp.int32)
    topk_p = min(INDEX_TOPK, tp // 4)
    topk_s = min(INDEX_TOPK, (past + ts) // 4)

    xp, xs = x_prompt, x_sample
    conv_p, conv_s = [], []
    kp, vp, ikp, ks_, vs_, iks = [], [], [], [], [], []
    ffn_p, ffn_s = [], []
    for i in range(DEPTH):
        j = i // N_MIXERS
        if i % N_MIXERS == 0:
            zero = jnp.zeros((bp, CONV_WIDTH - 1, D_MODEL), xp.dtype)
            mp, st_p = _short_conv_mixer(xp, zero, mix_w_in[j], mix_conv_w[j], mix_w_out[j])
            ms, st_s = _short_conv_mixer(xs, state_conv_mix[j], mix_w_in[j], mix_conv_w[j], mix_w_out[j])
            conv_p.append(st_p)
            conv_s.append(st_s)
        else:
            q, k, v, iq, ik, iw = _attn_project(xp, pos_p, attn_w_in[j], idx_k_norm_g[j], idx_k_norm_b[j])
            mp = _dsa_prompt(q, iq, iw, k, v, ik, topk_p) @ attn_w_out[j]
            kp.append(k)
            vp.append(v)
            ikp.append(ik)
            q, k, v, iq, ik, iw = _attn_project(xs, pos_s, attn_w_in[j], idx_k_norm_g[j], idx_k_norm_b[j])
            k_all = jnp.concatenate([cache_k[j], k], axis=1)
            v_all = jnp.concatenate([cache_v[j], v], axis=1)
            ik_all = jnp.concatenate([cache_idx_k[j], ik], axis=1)
            ms = _dsa_sample(q, iq, iw, k_all, v_all, ik_all, pos_s, topk_s) @ attn_w_out[j]
            ks_.append(k)
            vs_.append(v)
            iks.append(ik)
        xp = _layer_norm(DEEPNORM_ALPHA * xp + mp, ln1_g[i], ln1_b[i])
        xs = _layer_norm(DEEPNORM_ALPHA * xs + ms, ln1_g[i], ln1_b[i])
        zero_f = jnp.zeros((bp, CONV_WIDTH - 1, D_FF), xp.dtype)
        fp, fst_p = _conv_glu(xp, zero_f, ffn_w_in[i], ffn_conv_w[i], ffn_conv_b[i], ffn_w_down[i])
        fs, fst_s = _conv_glu(xs, state_ffn_conv[i], ffn_w_in[i], ffn_conv_w[i], ffn_conv_b[i], ffn_w_down[i])
        ffn_p.append(fst_p)
        ffn_s.append(fst_s)
        xp = _layer_norm(DEEPNORM_ALPHA * xp + fp, ln2_g[i], ln2_b[i])
        xs = _layer_norm(DEEPNORM_ALPHA * xs + fs, ln2_g[i], ln2_b[i])

    new_conv_mix_p = jnp.stack(conv_p)
    new_k_p = jnp.stack(kp)
    new_v_p = jnp.stack(vp)
    new_idx_k_p = jnp.stack(ikp)
    new_ffn_conv_p = jnp.stack(ffn_p)
    new_conv_mix_s = jnp.stack(conv_s)
    new_k_s = jnp.stack(ks_)
    new_v_s = jnp.stack(vs_)
    new_idx_k_s = jnp.stack(iks)
    new_ffn_conv_s = jnp.stack(ffn_s)
    return (xp, xs, new_conv_mix_p, new_k_p, new_v_p, new_idx_k_p, new_ffn_conv_p,
            new_conv_mix_s, new_k_s, new_v_s, new_idx_k_s, new_ffn_conv_s)
```

```python
import contextlib
import numpy as np
import concourse.bass as bass
import concourse.mybir as mybir
from concourse.bass_utils import run_bass_kernel_spmd

F32 = mybir.dt.float32
BF16 = mybir.dt.bfloat16
AF = mybir.ActivationFunctionType
ALU = mybir.AluOpType
AX = mybir.AxisListType

NCORES = 8
D = 2048
KC = 16
DFF = 5632
FC = 44
NMAIN = 1024
NSEQ = 4
TS = 64
HALO = 8
T = NMAIN + NSEQ * TS + HALO
TOK_S0 = NMAIN
TOK_H0 = NMAIN + NSEQ * TS
TT = [(0, 512), (512, 1024), (1024, T)]
QT = [(i * 128, 128) for i in range(10)] + [(TOK_H0, HALO)]
SEQ = 4096
PAST = 2048
NH = 16
NKV = 4
HD = 128
ALPHA = 4 ** 0.25
LN_EPS = 1e-5
CW = NMAIN + 2 + NSEQ * (TS + 2) + HALO + 2
CB_S0 = NMAIN + 2
CB_H0 = CB_S0 + NSEQ * (TS + 2)
AW = 5264
KVW = 1152
ATTN_SCALE = 128 ** -0.5
IDX_W_SCALE = (16 ** -0.5) * (128 ** -0.5)
TOPK = 256
NBIS = 26
NEG = -1.0e30
MASK_BIG = 30000.0


class Buf:
    __slots__ = ("name", "w", "r", "excl")

    def __init__(self, name, excl=False):
        self.name = name
        self.w = None
        self.r = {}
        self.excl = excl


class Sem:
    __slots__ = ("h", "val")

    def __init__(self, h):
        self.h = h
        self.val = 0


class KB:
    def __init__(self, nc, es):
        self.nc = nc
        self.E = {}
        for name, eng in [("pe", nc.tensor), ("act", nc.scalar), ("dve", nc.vector),
                          ("pool", nc.gpsimd), ("sp", nc.sync)]:
            s = Sem(es.enter_context(nc.semaphore("sem_" + name)))
            self.E[name] = dict(eng=eng, sem=s, waited={})
        self.dpools = {}
        for pname, n in [("w", 8), ("g", 16)]:
            self.dpools[pname] = [[Sem(es.enter_context(nc.semaphore("dsem_%s%d" % (pname, i)))) for i in range(n)], 0]
        self.ccsem = Sem(es.enter_context(nc.semaphore("ccsem")))

    def _wait(self, ename, evs):
        E = self.E[ename]
        need = {}
        for (s, v) in evs:
            if v > need.get(s, 0):
                need[s] = v
        for s, v in need.items():
            if E["waited"].get(s, 0) >= v:
                continue
            E["eng"].wait_ge(s.h, v)
            E["waited"][s] = v

    def op(self, ename, fn, reads=(), writes=()):
        E = self.E[ename]
        own = E["sem"]
        evs = []
        pe = ename == "pe"
        for b in reads:
            if b.w is not None:
                if not (b.w[0] is own and pe):
                    evs.append(b.w)
            if b.excl:
                for s, v in b.r.items():
                    if s is not own:
                        evs.append((s, v))
        for b in writes:
            if b.w is not None and not (b.w[0] is own and pe):
                evs.append(b.w)
            for s, v in b.r.items():
                if not (s is own and pe):
                    evs.append((s, v))
        self._wait(ename, evs)
        ins = fn()
        own.val += 1
        ins.then_inc(own.h, 1)
        ev = (own, own.val)
        for b in writes:
            b.w = ev
            b.r = {}
        for b in reads:
            b.r[own] = own.val
        return ins

    def dma(self, q, out, in_, reads=(), writes=(), pool="g"):
        E = self.E[q]
        evs = []
        for b in reads:
            if b.w is not None:
                evs.append(b.w)
        for b in writes:
            if b.w is not None:
                evs.append(b.w)
            for s, v in b.r.items():
                evs.append((s, v))
        P = self.dpools[pool]
        S = P[0][P[1] % len(P[0])]
        P[1] += 1
        if S.val > 0:
            evs.append((S, S.val))
        self._wait(q, evs)
        S.val += 16
        E["eng"].dma_start(out=out, in_=in_).then_inc(S.h, 16)
        ev = (S, S.val)
        for b in writes:
            b.w = ev
            b.r = {}
        for b in reads:
            b.r[S] = S.val

    def all_events(self):
        evs = []
        for e in self.E.values():
            if e["sem"].val > 0:
                evs.append((e["sem"], e["sem"].val))
        for P in self.dpools.values():
            for S in P[0]:
                if S.val > 0:
                    evs.append((S, S.val))
        if self.ccsem.val > 0:
            evs.append((self.ccsem, self.ccsem.val))
        return evs

    def barrier(self, engines=("pe", "act", "dve", "pool", "sp")):
        evs = self.all_events()
        for ename in engines:
            own = self.E[ename]["sem"]
            self._wait(ename, [e for e in evs if e[0] is not own])


class Pipe:
    def __init__(self, depth):
        self.depth = depth
        self.items = []

    def add(self, load, compute):
        self.items.append((load, compute))

    def run(self):
        n = len(self.items)
        for i in range(min(self.depth, n)):
            self.items[i][0]()
        for i in range(n):
            self.items[i][1]()
            if i + self.depth < n:
                self.items[i + self.depth][0]()


def tok_pieces(tt):
    if tt == 0:
        return [(0, 512, "c", 2)]
    if tt == 1:
        return [(0, 512, "c", 514)]
    return [(0, 256, "s", CB_S0 + 2), (256, 264, "c", CB_H0 + 2)]


def tview(ap2, lo, hi, kind):
    v = ap2[:, lo:hi]
    if kind == "s":
        v = v.rearrange("p (s c) -> p s c", c=TS)
    return v


def cview(cb, off, lo, hi, kind, col):
    c0 = col + off
    if kind == "s":
        return cb[:, c0:c0 + NSEQ * (TS + 2)].rearrange("p (s c) -> p s c", c=TS + 2)[:, :, 0:TS]
    return cb[:, c0:c0 + (hi - lo)]


def build_program(stop_after=None, dbg_gather=False):
    nc = bass.Bass("TRN2", target_bir_lowering=False)
    dt = nc.dram_tensor

    def din(name, shape, dtype=F32):
        return dt(name, list(shape), dtype, kind="ExternalInput").ap()

    def dout(name, shape, dtype=F32):
        return dt(name, list(shape), dtype, kind="ExternalOutput").ap()

    x_tok = din("x_tok", [T, D])
    hm_in = din("hm", [128, 1])
    ident_in = din("ident", [128, 128])
    st_mix = din("st_mix", [NSEQ * 2, D])
    st_ffn = din("st_ffn", [2, NSEQ * 2, DFF])
    tiny = stop_after == "p0"
    mix_w_in = din("mix_w_in", [D, 3 * D] if not tiny else [1, 1])
    mix_conv_w = din("mix_conv_w", [3, D])
    mix_w_out = din("mix_w_out", [D, D] if not tiny else [1, 1])
    ffn_w_in = din("ffn_w_in", [2, D, 2 * DFF] if not tiny else [1, 1, 1])
    ffn_conv_w = din("ffn_conv_w", [2, 3, DFF])
    ffn_conv_b = din("ffn_conv_b", [2, DFF])
    ffn_w_down = din("ffn_w_down", [2, DFF, D] if not tiny else [1, 1, 1])
    ln_g = din("ln_g", [4, D])
    ln_b = din("ln_b", [4, D])

    full = stop_after is None
    NOUT = NMAIN + NSEQ * TS
    if full:
        attn_w_in = din("attn_w_in", [D, AW])
        attn_w_out = din("attn_w_out", [D, D])
        kn_g = din("kn_g", [1, 128])
        kn_b = din("kn_b", [1, 128])
        cos_tab = din("cos_tab", [T, 64])
        sin_tab = din("sin_tab", [T, 64])
        kmax_tab = din("kmax_tab", [T, 1])
        iota_in = din("iota", [128, SEQ])
        cache_k = din("cache_k", [NSEQ, PAST, 512])
        cache_v = din("cache_v", [NSEQ, PAST, 512])
        cache_ik = din("cache_ik", [NSEQ, PAST, 128])
        k_out = dout("k_out", [NOUT, 512])
        v_out = dout("v_out", [NOUT, 512])
        ik_out = dout("ik_out", [NOUT, 128])
        kvi_own = [dt("kvi_own%d" % i, [128, KVW], F32) for i in range(8)]
        kvi_all = [dt("kvi_all%d" % i, [4 * 128, KVW], F32) for i in range(8)]
        kvi_all_in = din("kvi_all_in", [SEQ, KVW]) if dbg_gather else None
        kvi_samp = dt("kvi_samp", [NSEQ * TS, KVW], F32).ap()
        QTs = dt("QTs", [NH, 128, T], BF16).ap()
        IQTs = dt("IQTs", [NH, 128, T], BF16).ap()
        IWs = dt("IWs", [T, 32], F32).ap()
        ATs = dt("ATs", [NH, 128, T], BF16).ap()
        XSP = dt("XSP", [128, KC, T], F32).ap()
    y_out = dout("y_out", [NMAIN + NSEQ * TS, D])
    mixtail_out = dout("mixtail_out", [(1 + NSEQ) * 2, D])
    ffntail_out = dout("ffntail_out", [2, (1 + NSEQ) * 2, DFF])

    with contextlib.ExitStack() as es:
        kb = KB(nc, es)
        op, dma = kb.op, kb.dma

        def sb(name, shape, dtype=F32, stack=es):
            return stack.enter_context(nc.sbuf_tensor("s_" + name, list(shape), dtype))

        XR = [[Buf("xr%d_%d" % (k, t)) for t in range(3)] for k in range(KC)]
        XT = [[Buf("xt%d_%d" % (k, t)) for t in range(3)] for k in range(KC)]
        ident = sb("ident", [128, 128])
        ones = sb("ones", [128, 128])
        hm = sb("hm", [128, 1])
        lng = sb("lng", [128, 4, KC])
        lnb = sb("lnb", [128, 4, KC])
        cw_mix = sb("cw_mix", [128, 3, KC])
        cw_ffn = sb("cw_ffn", [128, 2, 3, FC])
        cb_ffn = sb("cb_ffn", [128, 2, FC])
        stm = sb("stm", [128, KC, NSEQ * 2])
        stf = sb("stf", [128, 2, FC, NSEQ * 2])
        B_const = Buf("const")
        KVB = {k: [Buf("kv_%s%d" % (k, i)) for i in range(len(QT))] for k in ("k", "v", "ik")}
        QTB = {k: [Buf("qt_%s%d" % (k, i)) for i in range(len(QT))] for k in ("q", "iq")}
        B_IW = Buf("iw")
        B_KVALL = Buf("kvall")
        B_AT = Buf("at")
        psum = es.enter_context(nc.psum_tensor("psum", [128, 8, 512], F32))
        PB = [Buf("bank%d" % i, excl=True) for i in range(8)]
        bank_ctr = [0]

        nbanks = [8]

        def next_bank():
            i = bank_ctr[0] % nbanks[0]
            bank_ctr[0] += 1
            return i

        NW = 6
        act = contextlib.ExitStack()
        xres = sb("xres", [128, KC, T], F32, act)
        xT = sb("xT", [128, KC, T], BF16, act)
        wring = sb("wring", [128, NW, KC, 128], BF16, act)
        WB = [Buf("w%d" % i) for i in range(NW)]
        w_ctr = [0]

        def next_w():
            i = w_ctr[0] % NW
            w_ctr[0] += 1
            return i

        def load_w(slot, src_rows_ap, nk):
            dma("pool", out=wring[:, slot, 0:nk, :], in_=src_rows_ap.rearrange("(k p) c -> p k c", p=128),
                writes=[WB[slot]], pool="w")

        dma("sp", out=ident[:], in_=ident_in[:, :], writes=[B_const])
        dma("sp", out=hm[:], in_=hm_in[:, :], writes=[B_const])
        op("dve", lambda: nc.vector.memset(ones[:], 1.0), writes=[B_const])

        def load_rows_T(dst_view_fn, src_ap, nrows, ncols, tag):
            with contextlib.ExitStack() as ls:
                stg = sb("rowstg_" + tag, [nrows, ncols], F32, ls)
                bstg = Buf("rowstg")
                dma("sp", out=stg[:], in_=src_ap, writes=[bstg])
                nch = ncols // 128
                for c0 in range(0, nch, 4):
                    nb = min(4, nch - c0)
                    bi = next_bank()

                    def f(c0=c0, nb=nb, bi=bi):
                        ins = None
                        for j in range(nb):
                            ins = nc.tensor.transpose(out=psum[:, bi, j * 128:j * 128 + nrows],
                                                      in_=stg[:, (c0 + j) * 128:(c0 + j + 1) * 128],
                                                      identity=ident[0:nrows, 0:nrows])
                        return ins
                    op("pe", f, reads=[bstg, B_const], writes=[PB[bi]])
                    for j in range(nb):
                        op("dve", lambda j=j, bi=bi, c0=c0: nc.vector.tensor_copy(
                            out=dst_view_fn(c0 + j), in_=psum[:, bi, j * 128:j * 128 + nrows]),
                            reads=[PB[bi]], writes=[B_const])
                kb.barrier()

        import os
        DBG = os.environ.get("KDBG", "")
        if "norows" in DBG:
            load_rows_T = lambda *a, **k: None
        load_rows_T(lambda c: lng[:, :, c], ln_g[:, :], 4, D, "lng")
        load_rows_T(lambda c: lnb[:, :, c], ln_b[:, :], 4, D, "lnb")
        load_rows_T(lambda c: cw_mix[:, :, c], mix_conv_w[:, :], 3, D, "cwm")
        load_rows_T(lambda c: stm[:, c, :], st_mix[:, :], NSEQ * 2, D, "stm")
        for l in range(2):
            load_rows_T(lambda c, l=l: cw_ffn[:, l, :, c], ffn_conv_w[l, :, :], 3, DFF, "cwf%d" % l)
            load_rows_T(lambda c, l=l: cb_ffn[:, l:l + 1, c], ffn_conv_b[l:l + 1, :], 1, DFF, "cbf%d" % l)
            load_rows_T(lambda c, l=l: stf[:, l, c, :], st_ffn[l, :, :], NSEQ * 2, DFF, "stf%d" % l)

        with contextlib.ExitStack() as ph:
            xstg = [sb("xstg%d" % i, [128, D], F32, ph) for i in range(2)]
            XS = [Buf("xstg%d" % i) for i in range(2)]
            for qi, (t0, n) in enumerate(QT if "nohalo" not in DBG else QT[:10]):
                s = qi % 2
                tt = min(t0 // 512, 2)
                dma("sp", out=xstg[s][0:n, :], in_=x_tok[t0:t0 + n, :], writes=[XS[s]])
                for g4 in range(4):
                    bi = next_bank()

                    def f(g4=g4, bi=bi, s=s, n=n):
                        ins = None
                        for j in range(4):
                            kc = g4 * 4 + j
                            ins = nc.tensor.transpose(out=psum[:, bi, j * 128:j * 128 + n],
                                                      in_=xstg[s][0:n, kc * 128:(kc + 1) * 128],
                                                      identity=ident[0:n, 0:n])
                        return ins
                    op("pe", f, reads=[XS[s], B_const], writes=[PB[bi]])
                    src = psum[:, bi, :].rearrange("p (j c) -> p j c", c=128)[:, :, 0:n]
                    op("act", lambda g4=g4, src=src, t0=t0, n=n: nc.scalar.copy(
                        out=xres[:, g4 * 4:g4 * 4 + 4, t0:t0 + n], in_=src),
                        reads=[PB[bi]], writes=[XR[g4 * 4 + j][tt] for j in range(4)])
                    op("dve", lambda g4=g4, src=src, t0=t0, n=n: nc.vector.tensor_copy(
                        out=xT[:, g4 * 4:g4 * 4 + 4, t0:t0 + n], in_=src),
                        reads=[PB[bi]], writes=[XT[g4 * 4 + j][tt] for j in range(4)])
            kb.barrier()

        def residual_add(first, m, tt, bi):
            lo, hi = TT[tt]
            n = hi - lo
            if first:
                op("dve", lambda: nc.vector.scalar_tensor_tensor(
                    out=xres[:, m, lo:hi], in0=xres[:, m, lo:hi], scalar=ALPHA, in1=psum[:, bi, 0:n],
                    op0=ALU.mult, op1=ALU.add), reads=[PB[bi], XR[m][tt]], writes=[XR[m][tt]])
            else:
                op("dve", lambda: nc.vector.tensor_tensor(
                    out=xres[:, m, lo:hi], in0=xres[:, m, lo:hi], in1=psum[:, bi, 0:n], op=ALU.add),
                    reads=[PB[bi], XR[m][tt]], writes=[XR[m][tt]])

        def layer_norm(idx, ph):
            sq = [sb("lnsq%d_%d" % (idx, i), [128, 512], F32, ph) for i in range(2)]
            SQ = [Buf("lnsq%d" % i) for i in range(2)]
            stat = sb("lnstat%d" % idx, [128, 4, 512], F32, ph)
            ST = Buf("lnstat")
            tmp = [sb("lntmp%d_%d" % (idx, i), [128, 512], F32, ph) for i in range(2)]
            TM = [Buf("lntmp%d" % i) for i in range(2)]
            for tt, (lo, hi) in enumerate(TT):
                n = hi - lo
                b_sum = next_bank()
                b_sq = next_bank()

                def fsum(b_sum=b_sum, lo=lo, hi=hi, n=n):
                    ins = None
                    for kc in range(KC):
                        ins = nc.tensor.matmul(psum[:, b_sum, 0:n], ones[:], xres[:, kc, lo:hi],
                                               start=(kc == 0), stop=(kc == KC - 1))
                    return ins
                op("pe", fsum, reads=[XR[kc][tt] for kc in range(KC)] + [B_const], writes=[PB[b_sum]])
                for kc in range(KC):
                    s = kc % 2
                    op("act", lambda s=s, kc=kc, lo=lo, hi=hi, n=n: nc.scalar.activation(
                        out=sq[s][:, 0:n], in_=xres[:, kc, lo:hi], func=AF.Square),
                        reads=[XR[kc][tt]], writes=[SQ[s]])
                    op("pe", lambda s=s, kc=kc, n=n, b_sq=b_sq: nc.tensor.matmul(
                        psum[:, b_sq, 0:n], ones[:], sq[s][:, 0:n], start=(kc == 0), stop=(kc == KC - 1)),
                        reads=[SQ[s], B_const], writes=[PB[b_sq]])
                mean, var, rstd, nmr = (stat[:, i, 0:n] for i in range(4))
                op("dve", lambda: nc.vector.tensor_scalar(out=mean, in0=psum[:, b_sum, 0:n], scalar1=1.0 / D,
                                                          scalar2=None, op0=ALU.mult),
                   reads=[PB[b_sum]], writes=[ST])
                op("dve", lambda: nc.vector.tensor_tensor(out=var, in0=mean, in1=mean, op=ALU.mult),
                   reads=[ST], writes=[ST])
                op("dve", lambda: nc.vector.scalar_tensor_tensor(out=var, in0=psum[:, b_sq, 0:n], scalar=1.0 / D,
                                                                 in1=var, op0=ALU.mult, op1=ALU.subtract),
                   reads=[PB[b_sq], ST], writes=[ST])
                op("dve", lambda: nc.vector.tensor_scalar(out=var, in0=var, scalar1=LN_EPS, scalar2=None,
                                                          op0=ALU.add),
                   reads=[ST], writes=[ST])
                op("act", lambda: nc.scalar.activation(out=var, in_=var, func=AF.Sqrt),
                   reads=[ST], writes=[ST])
                op("dve", lambda: nc.vector.reciprocal(out=rstd, in_=var),
                   reads=[ST], writes=[ST])
                op("dve", lambda: nc.vector.scalar_tensor_tensor(out=nmr, in0=mean, scalar=-1.0, in1=rstd,
                                                                 op0=ALU.mult, op1=ALU.mult),
                   reads=[ST], writes=[ST])
                for kc in range(KC):
                    s = kc % 2
                    op("dve", lambda s=s, kc=kc: nc.vector.tensor_tensor(
                        out=tmp[s][:, 0:n], in0=xres[:, kc, lo:hi], in1=rstd, op=ALU.mult),
                        reads=[XR[kc][tt], ST], writes=[TM[s]])
                    op("dve", lambda s=s: nc.vector.tensor_tensor(
                        out=tmp[s][:, 0:n], in0=tmp[s][:, 0:n], in1=nmr, op=ALU.add),
                        reads=[TM[s], ST], writes=[TM[s]])
                    op("act", lambda s=s, kc=kc: nc.scalar.activation(
                        out=xres[:, kc, lo:hi], in_=tmp[s][:, 0:n], func=AF.Identity,
                        scale=lng[:, idx, kc:kc + 1], bias=lnb[:, idx, kc:kc + 1]),
                        reads=[TM[s], B_const], writes=[XR[kc][tt]])
                    op("act", lambda s=s, kc=kc: nc.scalar.activation(
                        out=xT[:, kc, lo:hi], in_=tmp[s][:, 0:n], func=AF.Identity,
                        scale=lng[:, idx, kc:kc + 1], bias=lnb[:, idx, kc:kc + 1]),
                        reads=[TM[s], B_const], writes=[XT[kc][tt]])

        def conv_states(cb, CBb, st_view, tailbuf_view, TLb):
            op("dve", lambda: nc.vector.tensor_scalar(out=cb[:, 0:2], in0=cb[:, CW - 2:CW], scalar1=hm[:, 0:1],
                                                      scalar2=None, op0=ALU.mult),
               reads=[CBb, B_const], writes=[CBb])
            sv = cb[:, CB_S0:CB_S0 + NSEQ * (TS + 2)].rearrange("p (s c) -> p s c", c=TS + 2)
            op("dve", lambda: nc.vector.tensor_copy(out=sv[:, :, 0:2],
                                                    in_=st_view.rearrange("p (s r) -> p s r", r=2)),
               reads=[B_const, CBb], writes=[CBb])
            op("dve", lambda: nc.vector.tensor_copy(out=tailbuf_view[:, 0:2], in_=cb[:, NMAIN:NMAIN + 2]),
               reads=[CBb], writes=[TLb])
            op("dve", lambda: nc.vector.tensor_copy(
                out=tailbuf_view[:, 2:2 + 2 * NSEQ].rearrange("p (s r) -> p s r", r=2), in_=sv[:, :, TS:TS + 2]),
                reads=[CBb], writes=[TLb])

        def store_tails(tailbuf, nchunks, dst_ap, TLb, ph, tag):
            nr = (1 + NSEQ) * 2
            stgs = [sb("tailstg%d_%s" % (i, tag), [nr, 512], F32, ph) for i in range(2)]
            SG = [Buf("tailstg%d" % i) for i in range(2)]
            for ci, c0 in enumerate(range(0, nchunks, 4)):
                nb = min(4, nchunks - c0)
                bi = next_bank()
                si = ci % 2

                def f(c0=c0, nb=nb, bi=bi):
                    ins = None
                    for j in range(nb):
                        ins = nc.tensor.transpose(out=psum[0:nr, bi, j * 128:(j + 1) * 128],
                                                  in_=tailbuf[:, c0 + j, :], identity=ident[:, :])
                    return ins
                op("pe", f, reads=[TLb, B_const], writes=[PB[bi]])
                op("dve", lambda: nc.vector.tensor_copy(
                    out=stgs[si][:, 0:nb * 128], in_=psum[0:nr, bi, 0:nb * 128]),
                    reads=[PB[bi]], writes=[SG[si]])
                dma("sp", out=dst_ap[:, c0 * 128:(c0 + nb) * 128], in_=stgs[si][:, 0:nb * 128], reads=[SG[si]])

        def ln_phase(idx):
            with contextlib.ExitStack() as ph:
                layer_norm(idx, ph)
                kb.barrier()

        def mixer_phase():
            GS = 8
            with contextlib.ExitStack() as ph:
                zT = sb("zT", [128, GS, T], BF16, ph)
                ZB = [[Buf("z%d_%d" % (k, t)) for t in range(3)] for k in range(GS)]
                cbs = [sb("mcb%d" % i, [128, CW], F32, ph) for i in range(2)]
                CBb = [Buf("mcb%d" % i) for i in range(2)]
                ys = [sb("my", [128, CW - 2], F32, ph)] * 2
                YB = [Buf("my")] * 2
                bsb = [sb("mb%d" % i, [128, T], F32, ph) for i in range(2)]
                BB = [Buf("mb%d" % i) for i in range(2)]
                csb = [sb("mc%d" % i, [128, 512], F32, ph) for i in range(2)]
                CS = [Buf("mc%d" % i) for i in range(2)]
                tails = sb("mtails", [128, KC, (1 + NSEQ) * 2], F32, ph)
                TLb = Buf("mtails")
                for i in range(2):
                    op("dve", lambda i=i: nc.vector.memset(cbs[i][:], 0.0), writes=[CBb[i]])
                pipe = Pipe(1)
                cs_ctr = [0]
                for g in range(KC // GS):
                    for cc in range(GS):
                        c = g * GS + cc
                        slots = []

                        def load(c=c, slots=slots):
                            for j in range(3):
                                sl = next_w()
                                slots.append(sl)
                                load_w(sl, mix_w_in[:, j * D + c * 128: j * D + (c + 1) * 128], KC)

                        def compute(c=c, cc=cc, slots=slots):
                            p = c % 2
                            cb, y = cbs[p], ys[p]
                            for tt, (lo, hi) in enumerate(TT):
                                n = hi - lo
                                banks = []
                                for j in range(3):
                                    bi = next_bank()
                                    banks.append(bi)

                                    def f(bi=bi, sl=slots[j], lo=lo, hi=hi, n=n):
                                        ins = None
                                        for kc in range(KC):
                                            ins = nc.tensor.matmul(psum[:, bi, 0:n], wring[:, sl, kc, :],
                                                                   xT[:, kc, lo:hi],
                                                                   start=(kc == 0), stop=(kc == KC - 1))
                                        return ins
                                    op("pe", f, reads=[WB[slots[j]]] + [XT[kc][tt] for kc in range(KC)],
                                       writes=[PB[bi]])
                                bB, bC, bV = banks
                                op("act", lambda: nc.scalar.copy(out=bsb[p][:, lo:hi], in_=psum[:, bB, 0:n]),
                                   reads=[PB[bB]], writes=[BB[p]])
                                q = cs_ctr[0] % 2
                                cs_ctr[0] += 1
                                op("act", lambda: nc.scalar.copy(out=csb[q][:, 0:n], in_=psum[:, bC, 0:n]),
                                   reads=[PB[bC]], writes=[CS[q]])
                                for (a, b_, kind, col) in tok_pieces(tt):
                                    op("dve", lambda: nc.vector.tensor_tensor(
                                        out=cview(cb, 0, a, b_, kind, col), in0=tview(csb[q][:, 0:n], a, b_, kind),
                                        in1=tview(psum[:, bV, 0:n], a, b_, kind), op=ALU.mult),
                                        reads=[CS[q], PB[bV]], writes=[CBb[p]])
                            conv_states(cb, CBb[p], stm[:, c, :], tails[:, c, :], TLb)
                            op("dve", lambda: nc.vector.tensor_scalar(out=y[:], in0=cb[:, 0:CW - 2],
                                                                      scalar1=cw_mix[:, 0, c:c + 1], scalar2=None,
                                                                      op0=ALU.mult),
                               reads=[CBb[p], B_const], writes=[YB[p]])
                            for wi in (1, 2):
                                op("dve", lambda: nc.vector.scalar_tensor_tensor(
                                    out=y[:], in0=cb[:, wi:CW - 2 + wi], scalar=cw_mix[:, wi, c:c + 1], in1=y[:],
                                    op0=ALU.mult, op1=ALU.add), reads=[CBb[p], YB[p], B_const], writes=[YB[p]])
                            for tt, (lo, hi) in enumerate(TT):
                                for (a, b_, kind, col) in tok_pieces(tt):
                                    op("dve", lambda: nc.vector.tensor_tensor(
                                        out=tview(zT[:, cc, lo:hi], a, b_, kind),
                                        in0=tview(bsb[p][:, lo:hi], a, b_, kind),
                                        in1=cview(y, -2, a, b_, kind, col), op=ALU.mult),
                                        reads=[BB[p], YB[p]], writes=[ZB[cc][tt]])
                        pipe.add(load, compute)
                    for m in range(KC):
                        slots = []

                        def load(m=m, g=g, slots=slots):
                            sl = next_w()
                            slots.append(sl)
                            load_w(sl, mix_w_out[g * GS * 128:(g + 1) * GS * 128, m * 128:(m + 1) * 128], GS)

                        def compute(m=m, g=g, slots=slots):
                            sl = slots[0]
                            for tt, (lo, hi) in enumerate(TT):
                                n = hi - lo
                                bi = next_bank()

                                def f(bi=bi, lo=lo, hi=hi, n=n):
                                    ins = None
                                    for kc in range(GS):
                                        ins = nc.tensor.matmul(psum[:, bi, 0:n], wring[:, sl, kc, :], zT[:, kc, lo:hi],
                                                               start=(kc == 0), stop=(kc == GS - 1))
                                    return ins
                                op("pe", f, reads=[WB[sl]] + [ZB[kc][tt] for kc in range(GS)], writes=[PB[bi]])
                                residual_add(g == 0, m, tt, bi)
                        pipe.add(load, compute)
                pipe.run()
                store_tails(tails, KC, mixtail_out, TLb, ph, "mix")
                kb.barrier()
            ln_phase(0)

        def ffn_phase(l):
            GS = 8
            with contextlib.ExitStack() as ph:
                hT = sb("hT%d" % l, [128, GS, T], BF16, ph)
                HB = [[Buf("h%d_%d" % (k, t)) for t in range(3)] for k in range(GS)]
                cbs = [sb("fcb%d_%d" % (l, i), [128, CW], F32, ph) for i in range(2)]
                CBb = [Buf("fcb%d" % i) for i in range(2)]
                ys = [sb("fy%d" % l, [128, CW - 2], F32, ph)] * 2
                YB = [Buf("fy")] * 2
                usb = [sb("fu%d_%d" % (l, i), [128, T], F32, ph) for i in range(2)]
                UB = [Buf("fu%d" % i) for i in range(2)]
                tails = sb("ftails%d" % l, [128, FC, (1 + NSEQ) * 2], F32, ph)
                TLb = Buf("ftails")
                for i in range(2):
                    op("dve", lambda i=i: nc.vector.memset(cbs[i][:], 0.0), writes=[CBb[i]])
                pipe = Pipe(2)
                ngroups = (FC + GS - 1) // GS
                for g in range(ngroups):
                    j0 = g * GS
                    gn = min(GS, FC - j0)
                    for jj in range(gn):
                        j = j0 + jj
                        slots = []

                        def load(j=j, slots=slots):
                            for half in range(2):
                                sl = next_w()
                                slots.append(sl)
                                load_w(sl, ffn_w_in[l, :, half * DFF + j * 128: half * DFF + (j + 1) * 128], KC)

                        def compute(j=j, jj=jj, slots=slots):
                            p = j % 2
                            cb, y = cbs[p], ys[p]
                            for tt, (lo, hi) in enumerate(TT):
                                n = hi - lo
                                banks = []
                                for half in range(2):
                                    bi = next_bank()
                                    banks.append(bi)

                                    def f(bi=bi, sl=slots[half], lo=lo, hi=hi, n=n):
                                        ins = None
                                        for kc in range(KC):
                                            ins = nc.tensor.matmul(psum[:, bi, 0:n], wring[:, sl, kc, :],
                                                                   xT[:, kc, lo:hi],
                                                                   start=(kc == 0), stop=(kc == KC - 1))
                                        return ins
                                    op("pe", f, reads=[WB[slots[half]]] + [XT[kc][tt] for kc in range(KC)],
                                       writes=[PB[bi]])
                                bG, bU = banks
                                op("act", lambda: nc.scalar.copy(out=usb[p][:, lo:hi], in_=psum[:, bU, 0:n]),
                                   reads=[PB[bU]], writes=[UB[p]])
                                for (a, b_, kind, col) in tok_pieces(tt):
                                    op("act", lambda a=a, b_=b_, kind=kind, col=col: nc.scalar.copy(
                                        out=cview(cb, 0, a, b_, kind, col), in_=tview(psum[:, bG, 0:n], a, b_, kind)),
                                        reads=[PB[bG]], writes=[CBb[p]])
                            conv_states(cb, CBb[p], stf[:, l, j, :], tails[:, j, :], TLb)
                            op("dve", lambda: nc.vector.tensor_scalar(
                                out=y[:], in0=cb[:, 0:CW - 2], scalar1=cw_ffn[:, l, 0, j:j + 1],
                                scalar2=cb_ffn[:, l, j:j + 1], op0=ALU.mult, op1=ALU.add),
                                reads=[CBb[p], B_const], writes=[YB[p]])
                            for wi in (1, 2):
                                op("dve", lambda wi=wi: nc.vector.scalar_tensor_tensor(
                                    out=y[:], in0=cb[:, wi:CW - 2 + wi], scalar=cw_ffn[:, l, wi, j:j + 1], in1=y[:],
                                    op0=ALU.mult, op1=ALU.add), reads=[CBb[p], YB[p], B_const], writes=[YB[p]])
                            op("act", lambda: nc.scalar.activation(out=y[:], in_=y[:], func=AF.Silu),
                               reads=[YB[p]], writes=[YB[p]])
                            for tt, (lo, hi) in enumerate(TT):
                                for (a, b_, kind, col) in tok_pieces(tt):
                                    op("dve", lambda a=a, b_=b_, kind=kind, col=col, lo=lo, hi=hi:
                                       nc.vector.tensor_tensor(
                                           out=tview(hT[:, jj, lo:hi], a, b_, kind),
                                           in0=tview(usb[p][:, lo:hi], a, b_, kind),
                                           in1=cview(y, -2, a, b_, kind, col), op=ALU.mult),
                                       reads=[UB[p], YB[p]], writes=[HB[jj][tt]])
                        pipe.add(load, compute)
                    for m in range(KC):
                        slots = []

                        def load(m=m, j0=j0, gn=gn, slots=slots):
                            sl = next_w()
                            slots.append(sl)
                            load_w(sl, ffn_w_down[l, j0 * 128:(j0 + gn) * 128, m * 128:(m + 1) * 128], gn)

                        def compute(m=m, g=g, gn=gn, slots=slots):
                            sl = slots[0]
                            for tt, (lo, hi) in enumerate(TT):
                                n = hi - lo
                                bi = next_bank()

                                def f(bi=bi, lo=lo, hi=hi, n=n):
                                    ins = None
                                    for jj in range(gn):
                                        ins = nc.tensor.matmul(psum[:, bi, 0:n], wring[:, sl, jj, :], hT[:, jj, lo:hi],
                                                               start=(jj == 0), stop=(jj == gn - 1))
                                    return ins
                                op("pe", f, reads=[WB[sl]] + [HB[jj][tt] for jj in range(gn)], writes=[PB[bi]])
                                residual_add(g == 0, m, tt, bi)
                        pipe.add(load, compute)
                pipe.run()
                store_tails(tails, FC, ffntail_out[l, :, :], TLb, ph, "ffn%d" % l)
                kb.barrier()
            ln_phase(1 + 2 * l)

        def bcast_mid(ap2, n):
            lst = [list(x) for x in ap2.ap]
            return bass.AP(ap2.tensor, ap2.offset, [lst[0], [0, n]] + lst[1:])

        def attn_proj_phase():
            with contextlib.ExitStack() as ph:
                aw = sb("aw", [128, 2, KC, 512], BF16, ph)
                AWB = [Buf("aw0"), Buf("aw1")]
                cosb = sb("cosb", [128, len(QT), 64], F32, ph)
                sinb = sb("sinb", [128, len(QT), 64], F32, ph)
                kng = sb("kng", [128, 128], F32, ph)
                knb = sb("knb", [128, 128], F32, ph)
                pst = [sb("pst%d" % i, [128, 512], F32, ph) for i in range(2)]
                PS = [Buf("pst%d" % i) for i in range(2)]
                rop = [sb("rop%d" % i, [128, 512], F32, ph) for i in range(2)]
                RO = [Buf("rop%d" % i) for i in range(2)]
                tma = sb("tma", [128, 256], F32, ph)
                tmb = sb("tmb", [128, 256], F32, ph)
                TMb = Buf("tm")
                qst = [sb("qst%d" % i, [128, 4, 128], BF16, ph) for i in range(2)]
                QS = [Buf("qst%d" % i) for i in range(2)]
                st6 = sb("st6", [128, 6], F32, ph)
                mv = sb("mv", [128, 2], F32, ph)
                rs = sb("rs", [128, 1], F32, ph)
                iwst = [sb("iwst%d" % i, [128, 32], F32, ph) for i in range(2)]
                IWB = [Buf("iwst%d" % i) for i in range(2)]
                SMb = Buf("small")
                TB = Buf("tabs")
                for ti, (t0, n) in enumerate(QT):
                    dma("sp", out=cosb[0:n, ti, :], in_=cos_tab[t0:t0 + n, :], writes=[TB])
                    dma("sp", out=sinb[0:n, ti, :], in_=sin_tab[t0:t0 + n, :], writes=[TB])
                dma("sp", out=kng[:], in_=kn_g[0, :].partition_broadcast(128), writes=[TB])
                dma("sp", out=knb[:], in_=kn_b[0, :].partition_broadcast(128), writes=[TB])
                groups = [(512 * i, 512, "q", i) for i in range(4)] + [(2048, 512, "k", 0), (2560, 512, "v", 0)] + \
                         [(3072 + 512 * i, 512, "iq", i) for i in range(4)] + [(5120, 144, "ikw", 0)]
                ctr = [0]

                def rope(src, dst, n, ti, nh, sbuf_, dbuf_):
                    x = src.rearrange("p (h t d) -> p h t d", t=2, d=64)
                    o = dst.rearrange("p (h t d) -> p h t d", t=2, d=64)
                    c = bcast_mid(cosb[0:n, ti, :], nh)
                    sn = bcast_mid(sinb[0:n, ti, :], nh)
                    ta = tma[0:n, 0:nh * 64].rearrange("p (h d) -> p h d", d=64)
                    tb = tmb[0:n, 0:nh * 64].rearrange("p (h d) -> p h d", d=64)
                    V = nc.vector
                    op("dve", lambda: V.tensor_tensor(out=ta, in0=x[:, :, 0, :], in1=c, op=ALU.mult), reads=[TB, sbuf_], writes=[TMb])
                    op("dve", lambda: V.tensor_tensor(out=tb, in0=x[:, :, 1, :], in1=sn, op=ALU.mult), reads=[TB, sbuf_], writes=[TMb])
                    op("dve", lambda: V.tensor_tensor(out=o[:, :, 0, :], in0=ta, in1=tb, op=ALU.subtract), reads=[TMb], writes=[dbuf_])
                    op("dve", lambda: V.tensor_tensor(out=ta, in0=x[:, :, 0, :], in1=sn, op=ALU.mult), reads=[TB, sbuf_], writes=[TMb])
                    op("dve", lambda: V.tensor_tensor(out=tb, in0=x[:, :, 1, :], in1=c, op=ALU.mult), reads=[TB, sbuf_], writes=[TMb])
                    op("dve", lambda: V.tensor_tensor(out=o[:, :, 1, :], in0=ta, in1=tb, op=ALU.add), reads=[TMb], writes=[dbuf_])

                def out_rows(ti, t0, n, src_ap, col0, ncols, out_ap, srcbufs, kind):
                    if ti < 8:
                        dma("sp", out=kvi_own[ti].ap()[0:n, col0:col0 + ncols], in_=src_ap, reads=srcbufs,
                            writes=[KVB[kind][ti]])
                    elif ti < 10:
                        dma("sp", out=kvi_samp[t0 - NMAIN:t0 - NMAIN + n, col0:col0 + ncols], in_=src_ap, reads=srcbufs,
                            writes=[KVB[kind][ti]])
                    if ti < 10:
                        dma("sp", out=out_ap[t0:t0 + n, :], in_=src_ap, reads=srcbufs)

                pipe = Pipe(1)
                for gi, (col0, ncols, kind, idx) in enumerate(groups):
                    def load(gi=gi, col0=col0, ncols=ncols):
                        sl = gi % 2
                        dma("pool", out=aw[:, sl, :, 0:ncols],
                            in_=attn_w_in[:, col0:col0 + ncols].rearrange("(k p) c -> p k c", p=128),
                            writes=[AWB[sl]], pool="w")

                    def compute(gi=gi, col0=col0, ncols=ncols, kind=kind, idx=idx):
                        sl = gi % 2
                        pmap = {}

                        def front(ti):
                            t0, n = QT[ti]
                            tt = min(t0 // 512, 2)
                            bi = next_bank()

                            def f():
                                ins = None
                                for kc in range(KC):
                                    ins = nc.tensor.matmul(psum[0:n, bi, 0:ncols], xT[:, kc, t0:t0 + n],
                                                           aw[:, sl, kc, 0:ncols], start=(kc == 0), stop=(kc == KC - 1))
                                return ins
                            op("pe", f, reads=[AWB[sl]] + [XT[kc][tt] for kc in range(KC)], writes=[PB[bi]])
                            p = ctr[0] % 2
                            ctr[0] += 1
                            pmap[ti] = p
                            op("act", lambda: nc.scalar.copy(out=pst[p][0:n, 0:ncols], in_=psum[0:n, bi, 0:ncols]),
                               reads=[PB[bi]], writes=[PS[p]])

                        def back(ti):
                            t0, n = QT[ti]
                            p = pmap[ti]
                            if kind == "v":
                                out_rows(ti, t0, n, pst[p][0:n, 0:512], 512, 512, v_out, [PS[p]], "v")
                            elif kind in ("q", "iq", "k"):
                                rope(pst[p][0:n, 0:512], rop[p][0:n, 0:512], n, ti, 4, PS[p], RO[p])
                                if kind == "k":
                                    out_rows(ti, t0, n, rop[p][0:n, 0:512], 0, 512, k_out, [RO[p]], "k")
                                else:
                                    b2 = next_bank()

                                    def ft():
                                        ins = None
                                        for j in range(4):
                                            ins = nc.tensor.transpose(out=psum[:, b2, j * 128:j * 128 + n],
                                                                      in_=rop[p][0:n, j * 128:(j + 1) * 128],
                                                                      identity=ident[0:n, 0:n])
                                        return ins
                                    op("pe", ft, reads=[RO[p], B_const], writes=[PB[b2]])
                                    op("act", lambda: nc.scalar.copy(
                                        out=qst[p][:, :, 0:n],
                                        in_=psum[:, b2, :].rearrange("p (j c) -> p j c", c=128)[:, :, 0:n]),
                                        reads=[PB[b2]], writes=[QS[p]])
                                    dst = QTs if kind == "q" else IQTs
                                    dma("sp", out=dst[4 * idx:4 * idx + 4, :, t0:t0 + n].rearrange("h d q -> d h q"),
                                        in_=qst[p][:, :, 0:n], reads=[QS[p]], writes=[QTB[kind][ti]])
                            else:
                                V = nc.vector
                                x = pst[p][0:n, 0:128]
                                op("dve", lambda: V.bn_stats(out=st6[0:n, :], in_=x), reads=[PS[p]], writes=[SMb])
                                op("dve", lambda: V.bn_aggr(out=mv[0:n, :], in_=st6[0:n, :]), reads=[SMb], writes=[SMb])
                                op("dve", lambda: V.tensor_scalar(out=rs[0:n, :], in0=mv[0:n, 1:2], scalar1=LN_EPS,
                                                                  scalar2=None, op0=ALU.add), reads=[SMb], writes=[SMb])
                                op("act", lambda: nc.scalar.activation(out=rs[0:n, :], in_=rs[0:n, :], func=AF.Sqrt),
                                   reads=[SMb], writes=[SMb])
                                op("dve", lambda: V.reciprocal(out=rs[0:n, :], in_=rs[0:n, :]), reads=[SMb], writes=[SMb])
                                op("dve", lambda: V.tensor_scalar(out=x, in0=x, scalar1=mv[0:n, 0:1], scalar2=rs[0:n, 0:1],
                                                                  op0=ALU.subtract, op1=ALU.mult),
                                   reads=[SMb, PS[p]], writes=[PS[p]])
                                op("dve", lambda: V.tensor_tensor(out=x, in0=x, in1=kng[0:n, :], op=ALU.mult),
                                   reads=[PS[p], TB], writes=[PS[p]])
                                op("dve", lambda: V.tensor_tensor(out=x, in0=x, in1=knb[0:n, :], op=ALU.add),
                                   reads=[PS[p], TB], writes=[PS[p]])
                                rope(x, rop[p][0:n, 0:128], n, ti, 1, PS[p], RO[p])
                                out_rows(ti, t0, n, rop[p][0:n, 0:128], 1024, 128, ik_out, [RO[p]], "ik")
                                w = pst[p][0:n, 128:144]
                                op("dve", lambda: V.tensor_scalar(out=iwst[p][0:n, 16:32], in0=w, scalar1=0.0,
                                                                  scalar2=2.0, op0=ALU.is_ge, op1=ALU.mult),
                                   reads=[PS[p]], writes=[IWB[p]])
                                op("dve", lambda: V.tensor_scalar(out=iwst[p][0:n, 16:32], in0=iwst[p][0:n, 16:32],
                                                                  scalar1=-1.0, scalar2=None, op0=ALU.add),
                                   reads=[IWB[p]], writes=[IWB[p]])
                                op("dve", lambda: V.scalar_tensor_tensor(
                                    out=iwst[p][0:n, 0:16], in0=w, scalar=IDX_W_SCALE, in1=iwst[p][0:n, 16:32],
                                    op0=ALU.mult, op1=ALU.mult), reads=[PS[p], IWB[p]], writes=[IWB[p]])
                                dma("sp", out=IWs[t0:t0 + n, :], in_=iwst[p][0:n, :], reads=[IWB[p]], writes=[B_IW])
                        for ti in range(len(QT) + 1):
                            if ti < len(QT):
                                front(ti)
                            if ti >= 1:
                                back(ti - 1)
                    pipe.add(load, compute)
                pipe.run()
                kb.barrier()

        def gather_phase():
            if dbg_gather:
                return
            for ti in range(8):
                evs = [KVB[kind][ti].w for kind in ("k", "v", "ik") if KVB[kind][ti].w is not None]
                kb._wait("pool", evs)
                nc.gpsimd.collective_compute("AllGather", ALU.bypass, replica_groups=[[0, 1, 2, 3], [4, 5, 6, 7]],
                                             ins=[kvi_own[ti].ap().opt()],
                                             outs=[kvi_all[ti].ap().opt()]).then_inc(kb.ccsem.h)
                kb.ccsem.val += 1
            B_KVALL.w = (kb.ccsem, kb.ccsem.val)

        def attention_phase():
            def key_src(st):
                if dbg_gather:
                    return kvi_all_in[st * 128:(st + 1) * 128, :]
                r, k = st // 8, st % 8
                return kvi_all[k].ap()[r * 128:(r + 1) * 128, :]

            with contextlib.ExitStack() as ph:
                nbanks[0] = 6
                bO, bL = 6, 7
                LS = PAST + TS
                KCOLS = 2 * LS
                NST_S = (LS + 127) // 128
                KT = sb("KT", [128, NKV, KCOLS], BF16, ph)
                Vt = sb("Vt", [128, 2 * NST_S, 512], BF16, ph)
                ikT = sb("ikT", [128, KCOLS], BF16, ph)
                KEYS_K = Buf("keysk")
                KEYS_V = Buf("keysv")
                SK = [Buf("skeyk0"), Buf("skeyk1")]
                SV = [Buf("skeyv0"), Buf("skeyv1")]
                kst = [sb("kst%d" % i, [128, KVW], F32, ph) for i in range(2)]
                KS = [[Buf("kst%d_%d" % (i, j)) for j in range(3)] for i in range(2)]
                chunk_iota = sb("chunk_iota", [128, SEQ // 64], F32, ph)
                onesb = sb("onesb", [128, 128], BF16, ph)
                NB = 2
                qT = [sb("qT%d" % i, [128, NH, 128], BF16, ph) for i in range(NB)]
                QB = [Buf("q%d" % i) for i in range(NB)]
                iqT = [sb("iqT%d" % i, [128, NH, 128], BF16, ph) for i in range(NB)]
                IQB = [Buf("iq%d" % i) for i in range(NB)]
                iw = [sb("iw%d" % i, [128, 32], F32, ph) for i in range(NB)]
                diag = [sb("diag%d" % i, [128, NH, 128], BF16, ph) for i in range(NB)]
                DGB = [Buf("diag%d" % i) for i in range(NB)]
                identb = sb("identb", [128, 128], BF16, ph)
                kmx = sb("kmx", [128, 1], F32, ph)
                KMB = Buf("kmx")
                I = [sb("I%d" % i, [128, SEQ], F32, ph) for i in range(NB)]
                IB = [Buf("I%d" % i) for i in range(NB)]
                junk = sb("junk", [128, SEQ], BF16, ph)
                JB = Buf("junk")
                M = [sb("M%d" % i, [128, SEQ], BF16, ph) for i in range(NB)]
                MB = [Buf("M%d" % i) for i in range(NB)]
                pen = sb("pen", [128, SEQ // 64], F32, ph)
                NR = 4
                R = [sb("R%d" % i, [128, 512], BF16, ph) for i in range(NR)]
                RB = [Buf("R%d" % i) for i in range(NR)]
                I4 = sb("I4", [128, 512], BF16, ph)
                PT = [sb("PT%d" % i, [128, 512], BF16, ph) for i in range(3)]
                PTB = [Buf("PT%d" % i) for i in range(3)]
                aTq = [sb("aTq%d" % i, [128, NH, 128], BF16, ph) for i in range(NB)]
                ATB = [Buf("aTq%d" % i) for i in range(NB)]
                rl = sb("rl", [128, 512], F32, ph)
                RLB = Buf("rl")
                of = sb("of", [128, 512], F32, ph)
                OFB = Buf("of")
                onesf = sb("onesf", [128, 512], F32, ph)
                sc = [sb("sc%d" % i, [128, 8], F32, ph) for i in range(NB)]
                SCB = [Buf("sc%d" % i) for i in range(NB)]
                V = nc.vector
                dma("sp", out=chunk_iota[:], in_=iota_in[:, 0:SEQ // 64], writes=[B_const])
                op("dve", lambda: V.memset(onesb[:], 1.0), writes=[B_const])
                op("dve", lambda: V.tensor_copy(out=identb[:], in_=ident[:]), reads=[B_const], writes=[B_const])
                for j4 in range(4):
                    op("dve", lambda j4=j4: V.tensor_copy(out=I4[:, j4 * 128:(j4 + 1) * 128], in_=ident[:]),
                       reads=[B_const], writes=[B_const])
                op("dve", lambda: V.memset(onesf[:], -1.0), writes=[B_const])
                rctr = [0]
                pctr = [0]
                nbanks[0] = 4
                acc_ctr = [0]

                def stage_to_keys(s, nrows, st, col0, vt0, kbuf, vbuf):
                    b = next_bank()

                    def f():
                        ins = None
                        for j in range(4):
                            ins = nc.tensor.transpose(out=psum[:, b, j * 128:j * 128 + nrows],
                                                      in_=kst[s][0:nrows, j * 128:(j + 1) * 128],
                                                      identity=ident[0:nrows, 0:nrows])
                        return ins
                    op("pe", f, reads=KS[s] + [B_const], writes=[PB[b]])
                    c = col0 + st * 128
                    op("act", lambda: nc.scalar.copy(
                        out=KT[:, :, c:c + nrows],
                        in_=psum[:, b, :].rearrange("p (j c) -> p j c", c=128)[:, :, 0:nrows]),
                        reads=[PB[b]], writes=kbuf)
                    b2 = next_bank()
                    op("pe", lambda: nc.tensor.transpose(out=psum[:, b2, 0:nrows], in_=kst[s][0:nrows, 1024:1152],
                                                         identity=ident[0:nrows, 0:nrows]),
                       reads=KS[s] + [B_const], writes=[PB[b2]])
                    op("act", lambda: nc.scalar.copy(out=ikT[:, c:c + nrows], in_=psum[:, b2, 0:nrows]),
                       reads=[PB[b2]], writes=kbuf)
                    op("dve", lambda: V.tensor_copy(out=Vt[0:nrows, vt0 + st, :], in_=kst[s][0:nrows, 512:1024]),
                       reads=KS[s], writes=vbuf)

                class QTile:
                    pass

                def make_tile(i, tok0, nq, L, masked, col0, vt0, kbuf, vbuf):
                    t = QTile()
                    t.p = i % NB
                    t.tok0, t.nq, t.L, t.masked = tok0, nq, L, masked
                    t.col0, t.vt0, t.kbuf, t.vbuf = col0, vt0, kbuf, vbuf
                    t.nst = (L + 127) // 128
                    return t

                def index_stage(t):
                    p, nq, L, tok0 = t.p, t.nq, t.L, t.tok0
                    dma("sp", out=iqT[p][:, :, 0:nq], in_=IQTs[:, :, tok0:tok0 + nq].rearrange("h d q -> d h q"),
                        writes=[IQB[p]])
                    dma("sp", out=iw[p][0:nq, :], in_=IWs[tok0:tok0 + nq, :], writes=[IQB[p]])
                    for h in range(NH):
                        op("dve", lambda: V.tensor_scalar(out=diag[p][0:nq, h, 0:nq], in0=identb[0:nq, 0:nq],
                                                          scalar1=iw[p][0:nq, 16 + h:17 + h], scalar2=None,
                                                          op0=ALU.mult),
                           reads=[IQB[p], B_const], writes=[DGB[p]])
                    steps = []
                    for c0 in range(0, L, 512):
                        cn = min(512, L - c0)
                        ba = 4 + acc_ctr[0] % 2
                        acc_ctr[0] += 1
                        for h in range(NH):
                            steps.append((c0, cn, ba, h))
                    rmap = {}

                    def front(j):
                        c0, cn, ba, h = steps[j]
                        b = next_bank()
                        op("pe", lambda: nc.tensor.matmul(psum[0:nq, b, 0:cn], iqT[p][:, h, 0:nq],
                                                          ikT[:, t.col0 + c0:t.col0 + c0 + cn],
                                                          start=True, stop=True),
                           reads=[IQB[p]] + t.kbuf, writes=[PB[b]])
                        r = rctr[0] % NR
                        rctr[0] += 1
                        rmap[j] = r
                        op("act", lambda: nc.scalar.activation(out=R[r][0:nq, 0:cn], in_=psum[0:nq, b, 0:cn],
                                                               func=AF.Relu, scale=iw[p][0:nq, h:h + 1]),
                           reads=[PB[b], IQB[p]], writes=[RB[r]])

                    def back(j):
                        c0, cn, ba, h = steps[j]
                        r = rmap[j]
                        op("pe", lambda: nc.tensor.matmul(psum[0:nq, ba, 0:cn], diag[p][0:nq, h, 0:nq],
                                                          R[r][0:nq, 0:cn], start=(h == 0), stop=(h == NH - 1)),
                           reads=[DGB[p], RB[r]], writes=[PB[ba]])
                        if h == NH - 1:
                            op("act", lambda: nc.scalar.copy(out=I[p][0:nq, c0:c0 + cn], in_=psum[0:nq, ba, 0:cn]),
                               reads=[PB[ba]], writes=[IB[p]])
                    SKW = 2
                    for j in range(len(steps) + SKW):
                        if j < len(steps):
                            front(j)
                        if j >= SKW:
                            back(j - SKW)

                def bisect_stage(t):
                    p, nq, L = t.p, t.nq, t.L
                    Iv = I[p][0:nq, 0:L]
                    lo, w0, tt_, cnt, gek, hi = (sc[p][0:nq, i:i + 1] for i in range(6))
                    op("dve", lambda: V.tensor_reduce(out=lo, in_=Iv, axis=AX.X, op=ALU.min), reads=[IB[p]], writes=[SCB[p]])
                    op("dve", lambda: V.tensor_reduce(out=hi, in_=Iv, axis=AX.X, op=ALU.max), reads=[IB[p]], writes=[SCB[p]])
                    op("dve", lambda: V.scalar_tensor_tensor(out=w0, in0=hi, scalar=1.0, in1=lo, op0=ALU.add,
                                                             op1=ALU.subtract), reads=[SCB[p]], writes=[SCB[p]])
                    if t.masked:
                        dma("sp", out=kmx[0:nq, :], in_=kmax_tab[t.tok0:t.tok0 + nq, :], writes=[KMB])
                        op("dve", lambda: V.tensor_scalar(out=pen[0:nq, :], in0=chunk_iota[0:nq, :],
                                                          scalar1=kmx[0:nq, 0:1], scalar2=NEG,
                                                          op0=ALU.is_ge, op1=ALU.mult),
                           reads=[B_const, KMB], writes=[JB])
                        Iv3 = Iv.rearrange("p (c k) -> p c k", k=64)
                        pa = pen[0:nq, :]
                        pl = [list(x) for x in pa.ap]
                        pen_b = bass.AP(pa.tensor, pa.offset, [pl[0], pl[1], [0, 64]])
                        op("dve", lambda: V.tensor_tensor(out=Iv3, in0=Iv3, in1=pen_b, op=ALU.add),
                           reads=[IB[p], JB], writes=[IB[p]])
                    for it in range(NBIS):
                        wk = 0.5 ** (it + 1)
                        op("dve", lambda: V.scalar_tensor_tensor(out=tt_, in0=w0, scalar=wk, in1=lo, op0=ALU.mult,
                                                                 op1=ALU.add), reads=[SCB[p]], writes=[SCB[p]])
                        op("dve", lambda: V.tensor_scalar(out=junk[0:nq, 0:L], in0=Iv, scalar1=tt_, scalar2=0.0,
                                                          op0=ALU.is_ge, op1=ALU.add, accum_out=cnt),
                           reads=[IB[p], SCB[p]], writes=[JB, SCB[p]])
                        op("dve", lambda: V.tensor_scalar(out=gek, in0=cnt, scalar1=float(TOPK), scalar2=wk,
                                                          op0=ALU.is_ge, op1=ALU.mult), reads=[SCB[p]], writes=[SCB[p]])
                        op("dve", lambda: V.scalar_tensor_tensor(out=lo, in0=w0, scalar=gek, in1=lo, op0=ALU.mult,
                                                                 op1=ALU.add), reads=[SCB[p]], writes=[SCB[p]])
                    op("dve", lambda: V.tensor_scalar(out=M[p][0:nq, 0:L], in0=Iv, scalar1=lo, scalar2=-MASK_BIG,
                                                      op0=ALU.is_lt, op1=ALU.mult), reads=[IB[p], SCB[p]], writes=[MB[p]])

                def attend_stage(t):
                    p, nq, L, tok0 = t.p, t.nq, t.L, t.tok0
                    dma("sp", out=qT[p][:, :, 0:nq], in_=QTs[:, :, tok0:tok0 + nq].rearrange("h d q -> d h q"),
                        writes=[QB[p]])
                    steps = [(g, st) for g in range(NKV) for st in range(t.nst)]
                    kmap = {}

                    def front(j):
                        g, st = steps[j]
                        ns = min(128, L - st * 128)
                        b = next_bank()
                        c = t.col0 + st * 128
                        def fs():
                            o3 = psum[0:ns, b, 0:4 * nq].rearrange("p (h q) -> p h q", q=nq)
                            nc.tensor.matmul(o3, KT[:, g, c:c + ns], qT[p][:, 4 * g:4 * g + 4, 0:nq],
                                             start=True, stop=False)
                            return nc.tensor.matmul(o3, M[p][0:nq, st * 128:st * 128 + ns],
                                                    I4[0:nq, :].rearrange("p (h q) -> p h q", q=128)[:, :, 0:nq],
                                                    start=False, stop=True)
                        op("pe", fs, reads=t.kbuf + [QB[p], MB[p], B_const], writes=[PB[b]])
                        k = pctr[0] % 3
                        pctr[0] += 1
                        kmap[j] = k
                        op("act", lambda: nc.scalar.activation(out=PT[k][0:ns, 0:4 * nq], in_=psum[0:ns, b, 0:4 * nq],
                                                               func=AF.Exp, scale=ATTN_SCALE),
                           reads=[PB[b]], writes=[PTB[k]])

                    def back(j):
                        g, st = steps[j]
                        ns = min(128, L - st * 128)
                        k = kmap[j]
                        op("pe", lambda: nc.tensor.matmul(psum[:, bO, 0:4 * nq],
                                                          Vt[0:ns, t.vt0 + st, g * 128:(g + 1) * 128],
                                                          PT[k][0:ns, 0:4 * nq], start=(st == 0),
                                                          stop=(st == t.nst - 1)),
                           reads=t.vbuf + [PTB[k]], writes=[PB[bO]])
                        op("pe", lambda: nc.tensor.matmul(psum[:, bL, 0:4 * nq], onesb[0:ns, :],
                                                          PT[k][0:ns, 0:4 * nq], start=(st == 0),
                                                          stop=(st == t.nst - 1)),
                           reads=[B_const, PTB[k]], writes=[PB[bL]])
                        if st == t.nst - 1:
                            op("dve", lambda: V.reciprocal(out=rl[:, 0:4 * nq], in_=psum[:, bL, 0:4 * nq]),
                               reads=[PB[bL]], writes=[RLB])
                            op("dve", lambda: V.tensor_tensor(
                                out=aTq[p][:, 4 * g:4 * g + 4, 0:nq],
                                in0=psum[:, bO, 0:4 * nq].rearrange("p (h q) -> p h q", q=nq),
                                in1=rl[:, 0:4 * nq].rearrange("p (h q) -> p h q", q=nq), op=ALU.mult),
                                reads=[PB[bO], RLB], writes=[ATB[p]])
                    SKW = 2
                    for j in range(len(steps) + SKW):
                        if j < len(steps):
                            front(j)
                        if j >= SKW:
                            back(j - SKW)
                    dma("sp", out=ATs[:, :, tok0:tok0 + nq].rearrange("h d q -> d h q"), in_=aTq[p][:, :, 0:nq],
                        reads=[ATB[p]], writes=[B_AT])

                def load_prompt_keys():
                    for st in range(SEQ // 128):
                        s = st % 2
                        dma("sp", out=kst[s][:, :], in_=key_src(st), reads=[B_KVALL], writes=KS[s])
                        stage_to_keys(s, 128, st, 0, 0, [KEYS_K], [KEYS_V])

                def load_sample_keys(i):
                    j = i % 2
                    kbuf, vbuf = [SK[j], KEYS_K], [SV[j], KEYS_V]
                    for st in range(PAST // 128):
                        s = st % 2
                        dma("sp", out=kst[s][:, 0:512], in_=cache_k[i, st * 128:(st + 1) * 128, :], writes=[KS[s][0]])
                        dma("sp", out=kst[s][:, 512:1024], in_=cache_v[i, st * 128:(st + 1) * 128, :], writes=[KS[s][1]])
                        dma("sp", out=kst[s][:, 1024:1152], in_=cache_ik[i, st * 128:(st + 1) * 128, :], writes=[KS[s][2]])
                        stage_to_keys(s, 128, st, j * LS, j * NST_S, kbuf, vbuf)
                    st = PAST // 128
                    s = st % 2
                    dma("sp", out=kst[s][0:TS, :], in_=kvi_samp[i * TS:(i + 1) * TS, :], writes=KS[s])
                    stage_to_keys(s, TS, st, j * LS, j * NST_S, kbuf, vbuf)

                tiles = []
                for i, (t0, n) in enumerate(QT[:8] + [QT[10]]):
                    tiles.append(make_tile(i, t0, n, SEQ, True, 0, 0, [KEYS_K], [KEYS_V]))
                for i in range(NSEQ):
                    j = i % 2
                    tiles.append(make_tile(9 + i, NMAIN + i * TS, TS, LS, False, j * LS, j * NST_S, [SK[j]], [SV[j]]))
                load_prompt_keys()
                n = len(tiles)

                def emit_index(k):
                    if k >= 9:
                        load_sample_keys(k - 9)
                    index_stage(tiles[k])
                emit_index(0)
                for i in range(n + 1):
                    if i >= 1:
                        attend_stage(tiles[i - 1])
                    if i == 9:
                        emit_index(9)
                    if i + 1 < n and i + 1 != 9:
                        emit_index(i + 1)
                    if i < n:
                        bisect_stage(tiles[i])
                nbanks[0] = 8
                kb.barrier()

        def spill_xres():
            for kc in range(KC):
                dma("sp", out=XSP[:, kc, :], in_=xres[:, kc, :], reads=[XR[kc][t] for t in range(3)])
            kb.barrier()

        def reload_xres():
            for kc in range(KC):
                dma("sp", out=xres[:, kc, :], in_=XSP[:, kc, :], writes=[XR[kc][t] for t in range(3)])

        def attn_out_phase():
            with contextlib.ExitStack() as ph:
                aT = sb("aT", [128, KC, T], BF16, ph)
                AB = [Buf("aT%d" % h) for h in range(NH)]
                for h in range(NH):
                    dma("sp", out=aT[:, h, :], in_=ATs[h, :, :], writes=[AB[h]])
                pipe = Pipe(3)
                for m in range(KC):
                    slots = []

                    def load(m=m, slots=slots):
                        sl = next_w()
                        slots.append(sl)
                        load_w(sl, attn_w_out[:, m * 128:(m + 1) * 128], KC)

                    def compute(m=m, slots=slots):
                        sl = slots[0]
                        for tt, (lo, hi) in enumerate(TT):
                            n = hi - lo
                            bi = next_bank()

                            def f():
                                ins = None
                                for kc in range(KC):
                                    ins = nc.tensor.matmul(psum[:, bi, 0:n], wring[:, sl, kc, :], aT[:, kc, lo:hi],
                                                           start=(kc == 0), stop=(kc == KC - 1))
                                return ins
                            op("pe", f, reads=[WB[sl]] + AB, writes=[PB[bi]])
                            residual_add(True, m, tt, bi)
                    pipe.add(load, compute)
                pipe.run()
                kb.barrier()
            ln_phase(2)

        def output_phase():
            with contextlib.ExitStack() as ph:
                ostg = [sb("ostg%d" % i, [128, D], F32, ph) for i in range(2)]
                OS = [Buf("ostg%d" % i) for i in range(2)]
                for qi, (t0, n) in enumerate(QT[:10]):
                    s = qi % 2
                    tt = min(t0 // 512, 2)
                    for g4 in range(4):
                        bi = next_bank()

                        def f(g4=g4, bi=bi, t0=t0, n=n):
                            ins = None
                            for j in range(4):
                                kc = g4 * 4 + j
                                ins = nc.tensor.transpose(out=psum[0:n, bi, j * 128:(j + 1) * 128],
                                                          in_=xres[:, kc, t0:t0 + n], identity=ident[:, :])
                            return ins
                        op("pe", f, reads=[XR[g4 * 4 + j][tt] for j in range(4)] + [B_const], writes=[PB[bi]])
                        eng = "act" if g4 % 2 == 0 else "dve"
                        if eng == "act":
                            op("act", lambda g4=g4, bi=bi, s=s, n=n: nc.scalar.copy(
                                out=ostg[s][0:n, g4 * 512:(g4 + 1) * 512], in_=psum[0:n, bi, :]),
                                reads=[PB[bi]], writes=[OS[s]])
                        else:
                            op("dve", lambda g4=g4, bi=bi, s=s, n=n: nc.vector.tensor_copy(
                                out=ostg[s][0:n, g4 * 512:(g4 + 1) * 512], in_=psum[0:n, bi, :]),
                                reads=[PB[bi]], writes=[OS[s]])
                    dma("sp", out=y_out[t0:t0 + n, :], in_=ostg[s][0:n, :], reads=[OS[s]])
                kb.barrier()

        if stop_after not in ("p0",):
            mixer_phase()
        if stop_after not in ("p0", "mix"):
            ffn_phase(0)
        if full:
            attn_proj_phase()
            gather_phase()
            spill_xres()
            act.close()
            attention_phase()
            act = contextlib.ExitStack()
            xres = sb("xres_b", [128, KC, T], F32, act)
            xT = sb("xT_b", [128, KC, T], BF16, act)
            wring = sb("wring_b", [128, NW, KC, 128], BF16, act)
            reload_xres()
            attn_out_phase()
            ffn_phase(1)
        output_phase()
        kb.barrier()
        act.close()
        kb.barrier()
    return nc


_IDENT = np.eye(128, dtype=np.float32)
_IOTA = np.ascontiguousarray(np.broadcast_to(np.arange(SEQ, dtype=np.float32)[None, :], (128, SEQ)))


def _pos_tables(q):
    t0 = q * NMAIN
    pos = np.concatenate([t0 + np.arange(NMAIN), np.tile(PAST + np.arange(TS), NSEQ),
                          np.maximum(t0 - HALO + np.arange(HALO), 0)]).astype(np.int64)
    inv_freq = (np.float32(10000.0) ** (-np.arange(64, dtype=np.float32) / np.float32(64))).astype(np.float32)
    ang = pos.astype(np.float32)[:, None] * inv_freq[None, :]
    kmax = (pos // 64 + 1).astype(np.float32)[:, None]
    return np.cos(ang).astype(np.float32), np.sin(ang).astype(np.float32), np.ascontiguousarray(kmax)


def make_in_maps(inputs, full=True):
    g = lambda k: np.asarray(inputs[k])
    xp = g("x_prompt")
    xs = g("x_sample")
    in_maps = []
    for c in range(NCORES):
        b, q = c // 4, c % 4
        t0 = q * NMAIN
        main = xp[b, t0:t0 + NMAIN]
        if q == 0:
            halo = np.zeros((HALO, D), np.float32)
        else:
            halo = xp[b, t0 - HALO:t0]
        samp = xs[c * NSEQ:(c + 1) * NSEQ].reshape(NSEQ * TS, D)
        x_tok = np.ascontiguousarray(np.concatenate([main, samp, halo], axis=0))
        sl = slice(c * NSEQ, (c + 1) * NSEQ)
        m = {
            "x_tok": x_tok,
            "hm": np.full((128, 1), 0.0 if q == 0 else 1.0, np.float32),
            "ident": _IDENT,
            "st_mix": np.ascontiguousarray(g("state_conv_mix")[0, sl].reshape(NSEQ * 2, D)),
            "st_ffn": np.ascontiguousarray(g("state_ffn_conv")[:, sl].reshape(2, NSEQ * 2, DFF)),
            "mix_w_in": g("mix_w_in")[0],
            "mix_conv_w": g("mix_conv_w")[0],
            "mix_w_out": g("mix_w_out")[0],
            "ffn_w_in": g("ffn_w_in"),
            "ffn_conv_w": g("ffn_conv_w"),
            "ffn_conv_b": g("ffn_conv_b"),
            "ffn_w_down": g("ffn_w_down"),
            "ln_g": np.ascontiguousarray(np.stack([g("ln1_g")[0], g("ln2_g")[0], g("ln1_g")[1], g("ln2_g")[1]])),
            "ln_b": np.ascontiguousarray(np.stack([g("ln1_b")[0], g("ln2_b")[0], g("ln1_b")[1], g("ln2_b")[1]])),
        }
        if full:
            cos, sin, kmax = _pos_tables(q)
            m.update({
                "attn_w_in": g("attn_w_in")[0],
                "attn_w_out": g("attn_w_out")[0],
                "kn_g": g("idx_k_norm_g"),
                "kn_b": g("idx_k_norm_b"),
                "cos_tab": cos, "sin_tab": sin, "kmax_tab": kmax, "iota": _IOTA,
                "cache_k": np.ascontiguousarray(g("cache_k")[0, sl].reshape(NSEQ, PAST, 512)),
                "cache_v": np.ascontiguousarray(g("cache_v")[0, sl].reshape(NSEQ, PAST, 512)),
                "cache_ik": np.ascontiguousarray(g("cache_idx_k")[0, sl]),
            })
        in_maps.append(m)
    return in_maps


def assemble(results):
    BATCH, DEC = 2, 32
    y_p = np.zeros((BATCH, SEQ, D), np.float32)
    y_s = np.zeros((DEC, TS, D), np.float32)
    cm_p = np.zeros((1, BATCH, 2, D), np.float32)
    k_p = np.zeros((1, BATCH, SEQ, NKV, HD), np.float32)
    v_p = np.zeros((1, BATCH, SEQ, NKV, HD), np.float32)
    ik_p = np.zeros((1, BATCH, SEQ, 128), np.float32)
    ff_p = np.zeros((2, BATCH, 2, DFF), np.float32)
    cm_s = np.zeros((1, DEC, 2, D), np.float32)
    k_s = np.zeros((1, DEC, TS, NKV, HD), np.float32)
    v_s = np.zeros((1, DEC, TS, NKV, HD), np.float32)
    ik_s = np.zeros((1, DEC, TS, 128), np.float32)
    ff_s = np.zeros((2, DEC, 2, DFF), np.float32)
    for c in range(NCORES):
        r = results[c]
        b, q = c // 4, c % 4
        t0 = q * NMAIN
        sl = slice(c * NSEQ, (c + 1) * NSEQ)
        y = r["y_out"]
        y_p[b, t0:t0 + NMAIN] = y[:NMAIN]
        y_s[sl] = y[NMAIN:].reshape(NSEQ, TS, D)
        mt = r["mixtail_out"].reshape((1 + NSEQ) * 2, D)
        ft = r["ffntail_out"].reshape(2, (1 + NSEQ) * 2, DFF)
        cm_s[0, sl] = mt[2:].reshape(NSEQ, 2, D)
        ff_s[:, sl] = ft[:, 2:].reshape(2, NSEQ, 2, DFF)
        if q == 3:
            cm_p[0, b] = mt[:2]
            ff_p[:, b] = ft[:, :2]
        k_p[0, b, t0:t0 + NMAIN] = r["k_out"][:NMAIN].reshape(NMAIN, NKV, HD)
        v_p[0, b, t0:t0 + NMAIN] = r["v_out"][:NMAIN].reshape(NMAIN, NKV, HD)
        ik_p[0, b, t0:t0 + NMAIN] = r["ik_out"][:NMAIN]
        k_s[0, sl] = r["k_out"][NMAIN:].reshape(NSEQ, TS, NKV, HD)
        v_s[0, sl] = r["v_out"][NMAIN:].reshape(NSEQ, TS, NKV, HD)
        ik_s[0, sl] = r["ik_out"][NMAIN:].reshape(NSEQ, TS, 128)
    return (y_p, y_s, cm_p, k_p, v_p, ik_p, ff_p, cm_s, k_s, v_s, ik_s, ff_s)


def kernel(**inputs):
    nc = build_program()
    res = run_bass_kernel_spmd(nc, make_in_maps(inputs), core_ids=list(range(NCORES)))
    return assemble(res.results)
```

```python
import contextlib
import numpy as np
import concourse.bass as bass
import concourse.mybir as mybir
from concourse.bass_utils import run_bass_kernel_spmd

F32 = mybir.dt.float32
BF16 = mybir.dt.bfloat16
AF = mybir.ActivationFunctionType
ALU = mybir.AluOpType
AX = mybir.AxisListType

NCORES = 8
D = 2048
KC = 16
DFF = 5632
FC = 44
NMAIN = 1024
NSEQ = 4
TS = 64
HALO = 8
T = NMAIN + NSEQ * TS + HALO
TOK_S0 = NMAIN
TOK_H0 = NMAIN + NSEQ * TS
TT = [(0, 512), (512, 1024), (1024, T)]
QT = [(i * 128, 128) for i in range(10)] + [(TOK_H0, HALO)]
SEQ = 4096
PAST = 2048
NH = 16
NKV = 4
HD = 128
ALPHA = 4 ** 0.25
LN_EPS = 1e-5
CW = NMAIN + 2 + NSEQ * (TS + 2) + HALO + 2
CB_S0 = NMAIN + 2
CB_H0 = CB_S0 + NSEQ * (TS + 2)
AW = 5264
KVW = 1152
ATTN_SCALE = 128 ** -0.5
IDX_W_SCALE = (16 ** -0.5) * (128 ** -0.5)
TOPK = 256
NBIS = 26
NEG = -1.0e30


class Buf:
    __slots__ = ("name", "w", "r", "excl")

    def __init__(self, name, excl=False):
        self.name = name
        self.w = None
        self.r = {}
        self.excl = excl


class Sem:
    __slots__ = ("h", "val")

    def __init__(self, h):
        self.h = h
        self.val = 0


class KB:
    def __init__(self, nc, es):
        self.nc = nc
        self.E = {}
        for name, eng in [("pe", nc.tensor), ("act", nc.scalar), ("dve", nc.vector),
                          ("pool", nc.gpsimd), ("sp", nc.sync)]:
            s = Sem(es.enter_context(nc.semaphore("sem_" + name)))
            self.E[name] = dict(eng=eng, sem=s, waited={})
        self.dpools = {}
        for pname, n in [("w", 8), ("g", 16)]:
            self.dpools[pname] = [[Sem(es.enter_context(nc.semaphore("dsem_%s%d" % (pname, i)))) for i in range(n)], 0]
        self.ccsem = Sem(es.enter_context(nc.semaphore("ccsem")))

    def _wait(self, ename, evs):
        E = self.E[ename]
        need = {}
        for (s, v) in evs:
            if v > need.get(s, 0):
                need[s] = v
        for s, v in need.items():
            if E["waited"].get(s, 0) >= v:
                continue
            E["eng"].wait_ge(s.h, v)
            E["waited"][s] = v

    def op(self, ename, fn, reads=(), writes=()):
        E = self.E[ename]
        own = E["sem"]
        evs = []
        pe = ename == "pe"
        for b in reads:
            if b.w is not None:
                if not (b.w[0] is own and pe):
                    evs.append(b.w)
            if b.excl:
                for s, v in b.r.items():
                    if s is not own:
                        evs.append((s, v))
        for b in writes:
            if b.w is not None and not (b.w[0] is own and pe):
                evs.append(b.w)
            for s, v in b.r.items():
                if not (s is own and pe):
                    evs.append((s, v))
        self._wait(ename, evs)
        ins = fn()
        own.val += 1
        ins.then_inc(own.h, 1)
        ev = (own, own.val)
        for b in writes:
            b.w = ev
            b.r = {}
        for b in reads:
            b.r[own] = own.val
        return ins

    def dma(self, q, out, in_, reads=(), writes=(), pool="g"):
        E = self.E[q]
        evs = []
        for b in reads:
            if b.w is not None:
                evs.append(b.w)
        for b in writes:
            if b.w is not None:
                evs.append(b.w)
            for s, v in b.r.items():
                evs.append((s, v))
        P = self.dpools[pool]
        S = P[0][P[1] % len(P[0])]
        P[1] += 1
        if S.val > 0:
            evs.append((S, S.val))
        self._wait(q, evs)
        S.val += 16
        E["eng"].dma_start(out=out, in_=in_).then_inc(S.h, 16)
        ev = (S, S.val)
        for b in writes:
            b.w = ev
            b.r = {}
        for b in reads:
            b.r[S] = S.val

    def all_events(self):
        evs = []
        for e in self.E.values():
            if e["sem"].val > 0:
                evs.append((e["sem"], e["sem"].val))
        for P in self.dpools.values():
            for S in P[0]:
                if S.val > 0:
                    evs.append((S, S.val))
        if self.ccsem.val > 0:
            evs.append((self.ccsem, self.ccsem.val))
        return evs

    def barrier(self, engines=("pe", "act", "dve", "pool", "sp")):
        evs = self.all_events()
        for ename in engines:
            own = self.E[ename]["sem"]
            self._wait(ename, [e for e in evs if e[0] is not own])


class Pipe:
    def __init__(self, depth):
        self.depth = depth
        self.items = []

    def add(self, load, compute):
        self.items.append((load, compute))

    def run(self):
        n = len(self.items)
        for i in range(min(self.depth, n)):
            self.items[i][0]()
        for i in range(n):
            self.items[i][1]()
            if i + self.depth < n:
                self.items[i + self.depth][0]()


def tok_pieces(tt):
    if tt == 0:
        return [(0, 512, "c", 2)]
    if tt == 1:
        return [(0, 512, "c", 514)]
    return [(0, 256, "s", CB_S0 + 2), (256, 264, "c", CB_H0 + 2)]


def tview(ap2, lo, hi, kind):
    v = ap2[:, lo:hi]
    if kind == "s":
        v = v.rearrange("p (s c) -> p s c", c=TS)
    return v


def cview(cb, off, lo, hi, kind, col):
    c0 = col + off
    if kind == "s":
        return cb[:, c0:c0 + NSEQ * (TS + 2)].rearrange("p (s c) -> p s c", c=TS + 2)[:, :, 0:TS]
    return cb[:, c0:c0 + (hi - lo)]


def build_program(stop_after=None, dbg_gather=False):
    nc = bass.Bass("TRN2", target_bir_lowering=False)
    dt = nc.dram_tensor

    def din(name, shape, dtype=F32):
        return dt(name, list(shape), dtype, kind="ExternalInput").ap()

    def dout(name, shape, dtype=F32):
        return dt(name, list(shape), dtype, kind="ExternalOutput").ap()

    x_tok = din("x_tok", [T, D])
    hm_in = din("hm", [128, 1])
    ident_in = din("ident", [128, 128])
    st_mix = din("st_mix", [NSEQ * 2, D])
    st_ffn = din("st_ffn", [2, NSEQ * 2, DFF])
    tiny = stop_after == "p0"
    mix_w_in = din("mix_w_in", [D, 3 * D] if not tiny else [1, 1])
    mix_conv_w = din("mix_conv_w", [3, D])
    mix_w_out = din("mix_w_out", [D, D] if not tiny else [1, 1])
    ffn_w_in = din("ffn_w_in", [2, D, 2 * DFF] if not tiny else [1, 1, 1])
    ffn_conv_w = din("ffn_conv_w", [2, 3, DFF])
    ffn_conv_b = din("ffn_conv_b", [2, DFF])
    ffn_w_down = din("ffn_w_down", [2, DFF, D] if not tiny else [1, 1, 1])
    ln_g = din("ln_g", [4, D])
    ln_b = din("ln_b", [4, D])

    full = stop_after is None
    NOUT = NMAIN + NSEQ * TS
    if full:
        attn_w_in = din("attn_w_in", [D, AW])
        attn_w_out = din("attn_w_out", [D, D])
        kn_g = din("kn_g", [1, 128])
        kn_b = din("kn_b", [1, 128])
        cos_tab = din("cos_tab", [T, 64])
        sin_tab = din("sin_tab", [T, 64])
        kmax_tab = din("kmax_tab", [T, 1])
        iota_in = din("iota", [128, SEQ])
        cache_k = din("cache_k", [NSEQ, PAST, 512])
        cache_v = din("cache_v", [NSEQ, PAST, 512])
        cache_ik = din("cache_ik", [NSEQ, PAST, 128])
        k_out = dout("k_out", [NOUT, 512])
        v_out = dout("v_out", [NOUT, 512])
        ik_out = dout("ik_out", [NOUT, 128])
        kvi_own = [dt("kvi_own%d" % i, [128, KVW], F32) for i in range(8)]
        kvi_all = [dt("kvi_all%d" % i, [4 * 128, KVW], F32) for i in range(8)]
        kvi_all_in = din("kvi_all_in", [SEQ, KVW]) if dbg_gather else None
        kvi_samp = dt("kvi_samp", [NSEQ * TS, KVW], F32).ap()
        QTs = dt("QTs", [NH, 128, T], BF16).ap()
        IQTs = dt("IQTs", [NH, 128, T], BF16).ap()
        IWs = dt("IWs", [T, 32], F32).ap()
        ATs = dt("ATs", [NH, 128, T], BF16).ap()
        XSP = dt("XSP", [128, KC, T], F32).ap()
    y_out = dout("y_out", [NMAIN + NSEQ * TS, D])
    mixtail_out = dout("mixtail_out", [(1 + NSEQ) * 2, D])
    ffntail_out = dout("ffntail_out", [2, (1 + NSEQ) * 2, DFF])

    with contextlib.ExitStack() as es:
        kb = KB(nc, es)
        op, dma = kb.op, kb.dma

        def sb(name, shape, dtype=F32, stack=es):
            return stack.enter_context(nc.sbuf_tensor("s_" + name, list(shape), dtype))

        XR = [[Buf("xr%d_%d" % (k, t)) for t in range(3)] for k in range(KC)]
        XT = [[Buf("xt%d_%d" % (k, t)) for t in range(3)] for k in range(KC)]
        ident = sb("ident", [128, 128])
        ones = sb("ones", [128, 128])
        hm = sb("hm", [128, 1])
        lng = sb("lng", [128, 4, KC])
        lnb = sb("lnb", [128, 4, KC])
        cw_mix = sb("cw_mix", [128, 3, KC])
        cw_ffn = sb("cw_ffn", [128, 2, 3, FC])
        cb_ffn = sb("cb_ffn", [128, 2, FC])
        stm = sb("stm", [128, KC, NSEQ * 2])
        stf = sb("stf", [128, 2, FC, NSEQ * 2])
        B_const = Buf("const")
        KVB = {k: [Buf("kv_%s%d" % (k, i)) for i in range(len(QT))] for k in ("k", "v", "ik")}
        QTB = {k: [Buf("qt_%s%d" % (k, i)) for i in range(len(QT))] for k in ("q", "iq")}
        B_IW = Buf("iw")
        B_KVALL = Buf("kvall")
        B_AT = Buf("at")
        psum = es.enter_context(nc.psum_tensor("psum", [128, 8, 512], F32))
        PB = [Buf("bank%d" % i, excl=True) for i in range(8)]
        bank_ctr = [0]

        nbanks = [8]

        def next_bank():
            i = bank_ctr[0] % nbanks[0]
            bank_ctr[0] += 1
            return i

        NW = 6
        act = contextlib.ExitStack()
        xres = sb("xres", [128, KC, T], F32, act)
        xT = sb("xT", [128, KC, T], BF16, act)
        wring = sb("wring", [128, NW, KC, 128], BF16, act)
        WB = [Buf("w%d" % i) for i in range(NW)]
        w_ctr = [0]

        def next_w():
            i = w_ctr[0] % NW
            w_ctr[0] += 1
            return i

        def load_w(slot, src_rows_ap, nk):
            dma("pool", out=wring[:, slot, 0:nk, :], in_=src_rows_ap.rearrange("(k p) c -> p k c", p=128),
                writes=[WB[slot]], pool="w")

        dma("sp", out=ident[:], in_=ident_in[:, :], writes=[B_const])
        dma("sp", out=hm[:], in_=hm_in[:, :], writes=[B_const])
        op("dve", lambda: nc.vector.memset(ones[:], 1.0), writes=[B_const])

        def load_rows_T(dst_view_fn, src_ap, nrows, ncols, tag):
            with contextlib.ExitStack() as ls:
                stg = sb("rowstg_" + tag, [nrows, ncols], F32, ls)
                bstg = Buf("rowstg")
                dma("sp", out=stg[:], in_=src_ap, writes=[bstg])
                nch = ncols // 128
                for c0 in range(0, nch, 4):
                    nb = min(4, nch - c0)
                    bi = next_bank()

                    def f(c0=c0, nb=nb, bi=bi):
                        ins = None
                        for j in range(nb):
                            ins = nc.tensor.transpose(out=psum[:, bi, j * 128:j * 128 + nrows],
                                                      in_=stg[:, (c0 + j) * 128:(c0 + j + 1) * 128],
                                                      identity=ident[0:nrows, 0:nrows])
                        return ins
                    op("pe", f, reads=[bstg, B_const], writes=[PB[bi]])
                    for j in range(nb):
                        op("dve", lambda j=j, bi=bi, c0=c0: nc.vector.tensor_copy(
                            out=dst_view_fn(c0 + j), in_=psum[:, bi, j * 128:j * 128 + nrows]),
                            reads=[PB[bi]], writes=[B_const])
                kb.barrier()

        import os
        DBG = os.environ.get("KDBG", "")
        if "norows" in DBG:
            load_rows_T = lambda *a, **k: None
        load_rows_T(lambda c: lng[:, :, c], ln_g[:, :], 4, D, "lng")
        load_rows_T(lambda c: lnb[:, :, c], ln_b[:, :], 4, D, "lnb")
        load_rows_T(lambda c: cw_mix[:, :, c], mix_conv_w[:, :], 3, D, "cwm")
        load_rows_T(lambda c: stm[:, c, :], st_mix[:, :], NSEQ * 2, D, "stm")
        for l in range(2):
            load_rows_T(lambda c, l=l: cw_ffn[:, l, :, c], ffn_conv_w[l, :, :], 3, DFF, "cwf%d" % l)
            load_rows_T(lambda c, l=l: cb_ffn[:, l:l + 1, c], ffn_conv_b[l:l + 1, :], 1, DFF, "cbf%d" % l)
            load_rows_T(lambda c, l=l: stf[:, l, c, :], st_ffn[l, :, :], NSEQ * 2, DFF, "stf%d" % l)

        with contextlib.ExitStack() as ph:
            xstg = [sb("xstg%d" % i, [128, D], F32, ph) for i in range(2)]
            XS = [Buf("xstg%d" % i) for i in range(2)]
            for qi, (t0, n) in enumerate(QT if "nohalo" not in DBG else QT[:10]):
                s = qi % 2
                tt = min(t0 // 512, 2)
                dma("sp", out=xstg[s][0:n, :], in_=x_tok[t0:t0 + n, :], writes=[XS[s]])
                for g4 in range(4):
                    bi = next_bank()

                    def f(g4=g4, bi=bi, s=s, n=n):
                        ins = None
                        for j in range(4):
                            kc = g4 * 4 + j
                            ins = nc.tensor.transpose(out=psum[:, bi, j * 128:j * 128 + n],
                                                      in_=xstg[s][0:n, kc * 128:(kc + 1) * 128],
                                                      identity=ident[0:n, 0:n])
                        return ins
                    op("pe", f, reads=[XS[s], B_const], writes=[PB[bi]])
                    src = psum[:, bi, :].rearrange("p (j c) -> p j c", c=128)[:, :, 0:n]
                    op("act", lambda g4=g4, src=src, t0=t0, n=n: nc.scalar.copy(
                        out=xres[:, g4 * 4:g4 * 4 + 4, t0:t0 + n], in_=src),
                        reads=[PB[bi]], writes=[XR[g4 * 4 + j][tt] for j in range(4)])
                    op("dve", lambda g4=g4, src=src, t0=t0, n=n: nc.vector.tensor_copy(
                        out=xT[:, g4 * 4:g4 * 4 + 4, t0:t0 + n], in_=src),
                        reads=[PB[bi]], writes=[XT[g4 * 4 + j][tt] for j in range(4)])
            kb.barrier()

        def residual_add(first, m, tt, bi):
            lo, hi = TT[tt]
            n = hi - lo
            if first:
                op("dve", lambda: nc.vector.scalar_tensor_tensor(
                    out=xres[:, m, lo:hi], in0=xres[:, m, lo:hi], scalar=ALPHA, in1=psum[:, bi, 0:n],
                    op0=ALU.mult, op1=ALU.add), reads=[PB[bi], XR[m][tt]], writes=[XR[m][tt]])
            else:
                op("dve", lambda: nc.vector.tensor_tensor(
                    out=xres[:, m, lo:hi], in0=xres[:, m, lo:hi], in1=psum[:, bi, 0:n], op=ALU.add),
                    reads=[PB[bi], XR[m][tt]], writes=[XR[m][tt]])

        def layer_norm(idx, ph):
            sq = [sb("lnsq%d_%d" % (idx, i), [128, 512], F32, ph) for i in range(2)]
            SQ = [Buf("lnsq%d" % i) for i in range(2)]
            stat = sb("lnstat%d" % idx, [128, 4, 512], F32, ph)
            ST = Buf("lnstat")
            tmp = [sb("lntmp%d_%d" % (idx, i), [128, 512], F32, ph) for i in range(2)]
            TM = [Buf("lntmp%d" % i) for i in range(2)]
            for tt, (lo, hi) in enumerate(TT):
                n = hi - lo
                b_sum = next_bank()
                b_sq = next_bank()

                def fsum(b_sum=b_sum, lo=lo, hi=hi, n=n):
                    ins = None
                    for kc in range(KC):
                        ins = nc.tensor.matmul(psum[:, b_sum, 0:n], ones[:], xres[:, kc, lo:hi],
                                               start=(kc == 0), stop=(kc == KC - 1))
                    return ins
                op("pe", fsum, reads=[XR[kc][tt] for kc in range(KC)] + [B_const], writes=[PB[b_sum]])
                for kc in range(KC):
                    s = kc % 2
                    op("act", lambda s=s, kc=kc, lo=lo, hi=hi, n=n: nc.scalar.activation(
                        out=sq[s][:, 0:n], in_=xres[:, kc, lo:hi], func=AF.Square),
                        reads=[XR[kc][tt]], writes=[SQ[s]])
                    op("pe", lambda s=s, kc=kc, n=n, b_sq=b_sq: nc.tensor.matmul(
                        psum[:, b_sq, 0:n], ones[:], sq[s][:, 0:n], start=(kc == 0), stop=(kc == KC - 1)),
                        reads=[SQ[s], B_const], writes=[PB[b_sq]])
                mean, var, rstd, nmr = (stat[:, i, 0:n] for i in range(4))
                op("dve", lambda: nc.vector.tensor_scalar(out=mean, in0=psum[:, b_sum, 0:n], scalar1=1.0 / D,
                                                          scalar2=None, op0=ALU.mult),
                   reads=[PB[b_sum]], writes=[ST])
                op("dve", lambda: nc.vector.tensor_tensor(out=var, in0=mean, in1=mean, op=ALU.mult),
                   reads=[ST], writes=[ST])
                op("dve", lambda: nc.vector.scalar_tensor_tensor(out=var, in0=psum[:, b_sq, 0:n], scalar=1.0 / D,
                                                                 in1=var, op0=ALU.mult, op1=ALU.subtract),
                   reads=[PB[b_sq], ST], writes=[ST])
                op("dve", lambda: nc.vector.tensor_scalar(out=var, in0=var, scalar1=LN_EPS, scalar2=None,
                                                          op0=ALU.add),
                   reads=[ST], writes=[ST])
                op("act", lambda: nc.scalar.activation(out=var, in_=var, func=AF.Sqrt),
                   reads=[ST], writes=[ST])
                op("dve", lambda: nc.vector.reciprocal(out=rstd, in_=var),
                   reads=[ST], writes=[ST])
                op("dve", lambda: nc.vector.scalar_tensor_tensor(out=nmr, in0=mean, scalar=-1.0, in1=rstd,
                                                                 op0=ALU.mult, op1=ALU.mult),
                   reads=[ST], writes=[ST])
                for kc in range(KC):
                    s = kc % 2
                    op("dve", lambda s=s, kc=kc: nc.vector.tensor_tensor(
                        out=tmp[s][:, 0:n], in0=xres[:, kc, lo:hi], in1=rstd, op=ALU.mult),
                        reads=[XR[kc][tt], ST], writes=[TM[s]])
                    op("dve", lambda s=s: nc.vector.tensor_tensor(
                        out=tmp[s][:, 0:n], in0=tmp[s][:, 0:n], in1=nmr, op=ALU.add),
                        reads=[TM[s], ST], writes=[TM[s]])
                    op("act", lambda s=s, kc=kc: nc.scalar.activation(
                        out=xres[:, kc, lo:hi], in_=tmp[s][:, 0:n], func=AF.Identity,
                        scale=lng[:, idx, kc:kc + 1], bias=lnb[:, idx, kc:kc + 1]),
                        reads=[TM[s], B_const], writes=[XR[kc][tt]])
                    op("act", lambda s=s, kc=kc: nc.scalar.activation(
                        out=xT[:, kc, lo:hi], in_=tmp[s][:, 0:n], func=AF.Identity,
                        scale=lng[:, idx, kc:kc + 1], bias=lnb[:, idx, kc:kc + 1]),
                        reads=[TM[s], B_const], writes=[XT[kc][tt]])

        def conv_states(cb, CBb, st_view, tailbuf_view, TLb):
            op("dve", lambda: nc.vector.tensor_scalar(out=cb[:, 0:2], in0=cb[:, CW - 2:CW], scalar1=hm[:, 0:1],
                                                      scalar2=None, op0=ALU.mult),
               reads=[CBb, B_const], writes=[CBb])
            sv = cb[:, CB_S0:CB_S0 + NSEQ * (TS + 2)].rearrange("p (s c) -> p s c", c=TS + 2)
            op("dve", lambda: nc.vector.tensor_copy(out=sv[:, :, 0:2],
                                                    in_=st_view.rearrange("p (s r) -> p s r", r=2)),
               reads=[B_const, CBb], writes=[CBb])
            op("dve", lambda: nc.vector.tensor_copy(out=tailbuf_view[:, 0:2], in_=cb[:, NMAIN:NMAIN + 2]),
               reads=[CBb], writes=[TLb])
            op("dve", lambda: nc.vector.tensor_copy(
                out=tailbuf_view[:, 2:2 + 2 * NSEQ].rearrange("p (s r) -> p s r", r=2), in_=sv[:, :, TS:TS + 2]),
                reads=[CBb], writes=[TLb])

        def store_tails(tailbuf, nchunks, dst_ap, TLb, ph, tag):
            nr = (1 + NSEQ) * 2
            stgs = [sb("tailstg%d_%s" % (i, tag), [nr, 512], F32, ph) for i in range(2)]
            SG = [Buf("tailstg%d" % i) for i in range(2)]
            for ci, c0 in enumerate(range(0, nchunks, 4)):
                nb = min(4, nchunks - c0)
                bi = next_bank()
                si = ci % 2

                def f(c0=c0, nb=nb, bi=bi):
                    ins = None
                    for j in range(nb):
                        ins = nc.tensor.transpose(out=psum[0:nr, bi, j * 128:(j + 1) * 128],
                                                  in_=tailbuf[:, c0 + j, :], identity=ident[:, :])
                    return ins
                op("pe", f, reads=[TLb, B_const], writes=[PB[bi]])
                op("dve", lambda: nc.vector.tensor_copy(
                    out=stgs[si][:, 0:nb * 128], in_=psum[0:nr, bi, 0:nb * 128]),
                    reads=[PB[bi]], writes=[SG[si]])
                dma("sp", out=dst_ap[:, c0 * 128:(c0 + nb) * 128], in_=stgs[si][:, 0:nb * 128], reads=[SG[si]])

        def ln_phase(idx):
            with contextlib.ExitStack() as ph:
                layer_norm(idx, ph)
                kb.barrier()

        def mixer_phase():
            GS = 8
            with contextlib.ExitStack() as ph:
                zT = sb("zT", [128, GS, T], BF16, ph)
                ZB = [[Buf("z%d_%d" % (k, t)) for t in range(3)] for k in range(GS)]
                cbs = [sb("mcb%d" % i, [128, CW], F32, ph) for i in range(2)]
                CBb = [Buf("mcb%d" % i) for i in range(2)]
                ys = [sb("my", [128, CW - 2], F32, ph)] * 2
                YB = [Buf("my")] * 2
                bsb = [sb("mb%d" % i, [128, T], F32, ph) for i in range(2)]
                BB = [Buf("mb%d" % i) for i in range(2)]
                csb = [sb("mc%d" % i, [128, 512], F32, ph) for i in range(2)]
                CS = [Buf("mc%d" % i) for i in range(2)]
                tails = sb("mtails", [128, KC, (1 + NSEQ) * 2], F32, ph)
                TLb = Buf("mtails")
                for i in range(2):
                    op("dve", lambda i=i: nc.vector.memset(cbs[i][:], 0.0), writes=[CBb[i]])
                pipe = Pipe(1)
                cs_ctr = [0]
                for g in range(KC // GS):
                    for cc in range(GS):
                        c = g * GS + cc
                        slots = []

                        def load(c=c, slots=slots):
                            for j in range(3):
                                sl = next_w()
                                slots.append(sl)
                                load_w(sl, mix_w_in[:, j * D + c * 128: j * D + (c + 1) * 128], KC)

                        def compute(c=c, cc=cc, slots=slots):
                            p = c % 2
                            cb, y = cbs[p], ys[p]
                            for tt, (lo, hi) in enumerate(TT):
                                n = hi - lo
                                banks = []
                                for j in range(3):
                                    bi = next_bank()
                                    banks.append(bi)

                                    def f(bi=bi, sl=slots[j], lo=lo, hi=hi, n=n):
                                        ins = None
                                        for kc in range(KC):
                                            ins = nc.tensor.matmul(psum[:, bi, 0:n], wring[:, sl, kc, :],
                                                                   xT[:, kc, lo:hi],
                                                                   start=(kc == 0), stop=(kc == KC - 1))
                                        return ins
                                    op("pe", f, reads=[WB[slots[j]]] + [XT[kc][tt] for kc in range(KC)],
                                       writes=[PB[bi]])
                                bB, bC, bV = banks
                                op("act", lambda: nc.scalar.copy(out=bsb[p][:, lo:hi], in_=psum[:, bB, 0:n]),
                                   reads=[PB[bB]], writes=[BB[p]])
                                q = cs_ctr[0] % 2
                                cs_ctr[0] += 1
                                op("act", lambda: nc.scalar.copy(out=csb[q][:, 0:n], in_=psum[:, bC, 0:n]),
                                   reads=[PB[bC]], writes=[CS[q]])
                                for (a, b_, kind, col) in tok_pieces(tt):
                                    op("dve", lambda: nc.vector.tensor_tensor(
                                        out=cview(cb, 0, a, b_, kind, col), in0=tview(csb[q][:, 0:n], a, b_, kind),
                                        in1=tview(psum[:, bV, 0:n], a, b_, kind), op=ALU.mult),
                                        reads=[CS[q], PB[bV]], writes=[CBb[p]])
                            conv_states(cb, CBb[p], stm[:, c, :], tails[:, c, :], TLb)
                            op("dve", lambda: nc.vector.tensor_scalar(out=y[:], in0=cb[:, 0:CW - 2],
                                                                      scalar1=cw_mix[:, 0, c:c + 1], scalar2=None,
                                                                      op0=ALU.mult),
                               reads=[CBb[p], B_const], writes=[YB[p]])
                            for wi in (1, 2):
                                op("dve", lambda: nc.vector.scalar_tensor_tensor(
                                    out=y[:], in0=cb[:, wi:CW - 2 + wi], scalar=cw_mix[:, wi, c:c + 1], in1=y[:],
                                    op0=ALU.mult, op1=ALU.add), reads=[CBb[p], YB[p], B_const], writes=[YB[p]])
                            for tt, (lo, hi) in enumerate(TT):
                                for (a, b_, kind, col) in tok_pieces(tt):
                                    op("dve", lambda: nc.vector.tensor_tensor(
                                        out=tview(zT[:, cc, lo:hi], a, b_, kind),
                                        in0=tview(bsb[p][:, lo:hi], a, b_, kind),
                                        in1=cview(y, -2, a, b_, kind, col), op=ALU.mult),
                                        reads=[BB[p], YB[p]], writes=[ZB[cc][tt]])
                        pipe.add(load, compute)
                    for m in range(KC):
                        slots = []

                        def load(m=m, g=g, slots=slots):
                            sl = next_w()
                            slots.append(sl)
                            load_w(sl, mix_w_out[g * GS * 128:(g + 1) * GS * 128, m * 128:(m + 1) * 128], GS)

                        def compute(m=m, g=g, slots=slots):
                            sl = slots[0]
                            for tt, (lo, hi) in enumerate(TT):
                                n = hi - lo
                                bi = next_bank()

                                def f(bi=bi, lo=lo, hi=hi, n=n):
                                    ins = None
                                    for kc in range(GS):
                                        ins = nc.tensor.matmul(psum[:, bi, 0:n], wring[:, sl, kc, :], zT[:, kc, lo:hi],
                                                               start=(kc == 0), stop=(kc == GS - 1))
                                    return ins
                                op("pe", f, reads=[WB[sl]] + [ZB[kc][tt] for kc in range(GS)], writes=[PB[bi]])
                                residual_add(g == 0, m, tt, bi)
                        pipe.add(load, compute)
                pipe.run()
                store_tails(tails, KC, mixtail_out, TLb, ph, "mix")
                kb.barrier()
            ln_phase(0)

        def ffn_phase(l):
            GS = 8
            with contextlib.ExitStack() as ph:
                hT = sb("hT%d" % l, [128, GS, T], BF16, ph)
                HB = [[Buf("h%d_%d" % (k, t)) for t in range(3)] for k in range(GS)]
                cbs = [sb("fcb%d_%d" % (l, i), [128, CW], F32, ph) for i in range(2)]
                CBb = [Buf("fcb%d" % i) for i in range(2)]
                ys = [sb("fy%d" % l, [128, CW - 2], F32, ph)] * 2
                YB = [Buf("fy")] * 2
                usb = [sb("fu%d_%d" % (l, i), [128, T], F32, ph) for i in range(2)]
                UB = [Buf("fu%d" % i) for i in range(2)]
                tails = sb("ftails%d" % l, [128, FC, (1 + NSEQ) * 2], F32, ph)
                TLb = Buf("ftails")
                for i in range(2):
                    op("dve", lambda i=i: nc.vector.memset(cbs[i][:], 0.0), writes=[CBb[i]])
                pipe = Pipe(2)
                ngroups = (FC + GS - 1) // GS
                for g in range(ngroups):
                    j0 = g * GS
                    gn = min(GS, FC - j0)
                    for jj in range(gn):
                        j = j0 + jj
                        slots = []

                        def load(j=j, slots=slots):
                            for half in range(2):
                                sl = next_w()
                                slots.append(sl)
                                load_w(sl, ffn_w_in[l, :, half * DFF + j * 128: half * DFF + (j + 1) * 128], KC)

                        def compute(j=j, jj=jj, slots=slots):
                            p = j % 2
                            cb, y = cbs[p], ys[p]
                            for tt, (lo, hi) in enumerate(TT):
                                n = hi - lo
                                banks = []
                                for half in range(2):
                                    bi = next_bank()
                                    banks.append(bi)

                                    def f(bi=bi, sl=slots[half], lo=lo, hi=hi, n=n):
                                        ins = None
                                        for kc in range(KC):
                                            ins = nc.tensor.matmul(psum[:, bi, 0:n], wring[:, sl, kc, :],
                                                                   xT[:, kc, lo:hi],
                                                                   start=(kc == 0), stop=(kc == KC - 1))
                                        return ins
                                    op("pe", f, reads=[WB[slots[half]]] + [XT[kc][tt] for kc in range(KC)],
                                       writes=[PB[bi]])
                                bG, bU = banks
                                op("act", lambda: nc.scalar.copy(out=usb[p][:, lo:hi], in_=psum[:, bU, 0:n]),
                                   reads=[PB[bU]], writes=[UB[p]])
                                for (a, b_, kind, col) in tok_pieces(tt):
                                    op("act", lambda a=a, b_=b_, kind=kind, col=col: nc.scalar.copy(
                                        out=cview(cb, 0, a, b_, kind, col), in_=tview(psum[:, bG, 0:n], a, b_, kind)),
                                        reads=[PB[bG]], writes=[CBb[p]])
                            conv_states(cb, CBb[p], stf[:, l, j, :], tails[:, j, :], TLb)
                            op("dve", lambda: nc.vector.tensor_scalar(
                                out=y[:], in0=cb[:, 0:CW - 2], scalar1=cw_ffn[:, l, 0, j:j + 1],
                                scalar2=cb_ffn[:, l, j:j + 1], op0=ALU.mult, op1=ALU.add),
                                reads=[CBb[p], B_const], writes=[YB[p]])
                            for wi in (1, 2):
                                op("dve", lambda wi=wi: nc.vector.scalar_tensor_tensor(
                                    out=y[:], in0=cb[:, wi:CW - 2 + wi], scalar=cw_ffn[:, l, wi, j:j + 1], in1=y[:],
                                    op0=ALU.mult, op1=ALU.add), reads=[CBb[p], YB[p], B_const], writes=[YB[p]])
                            op("act", lambda: nc.scalar.activation(out=y[:], in_=y[:], func=AF.Silu),
                               reads=[YB[p]], writes=[YB[p]])
                            for tt, (lo, hi) in enumerate(TT):
                                for (a, b_, kind, col) in tok_pieces(tt):
                                    op("dve", lambda a=a, b_=b_, kind=kind, col=col, lo=lo, hi=hi:
                                       nc.vector.tensor_tensor(
                                           out=tview(hT[:, jj, lo:hi], a, b_, kind),
                                           in0=tview(usb[p][:, lo:hi], a, b_, kind),
                                           in1=cview(y, -2, a, b_, kind, col), op=ALU.mult),
                                       reads=[UB[p], YB[p]], writes=[HB[jj][tt]])
                        pipe.add(load, compute)
                    for m in range(KC):
                        slots = []

                        def load(m=m, j0=j0, gn=gn, slots=slots):
                            sl = next_w()
                            slots.append(sl)
                            load_w(sl, ffn_w_down[l, j0 * 128:(j0 + gn) * 128, m * 128:(m + 1) * 128], gn)

                        def compute(m=m, g=g, gn=gn, slots=slots):
                            sl = slots[0]
                            for tt, (lo, hi) in enumerate(TT):
                                n = hi - lo
                                bi = next_bank()

                                def f(bi=bi, lo=lo, hi=hi, n=n):
                                    ins = None
                                    for jj in range(gn):
                                        ins = nc.tensor.matmul(psum[:, bi, 0:n], wring[:, sl, jj, :], hT[:, jj, lo:hi],
                                                               start=(jj == 0), stop=(jj == gn - 1))
                                    return ins
                                op("pe", f, reads=[WB[sl]] + [HB[jj][tt] for jj in range(gn)], writes=[PB[bi]])
                                residual_add(g == 0, m, tt, bi)
                        pipe.add(load, compute)
                pipe.run()
                store_tails(tails, FC, ffntail_out[l, :, :], TLb, ph, "ffn%d" % l)
                kb.barrier()
            ln_phase(1 + 2 * l)

        def bcast_mid(ap2, n):
            lst = [list(x) for x in ap2.ap]
            return bass.AP(ap2.tensor, ap2.offset, [lst[0], [0, n]] + lst[1:])

        def attn_proj_phase():
            with contextlib.ExitStack() as ph:
                aw = sb("aw", [128, 2, KC, 512], BF16, ph)
                AWB = [Buf("aw0"), Buf("aw1")]
                cosb = sb("cosb", [128, len(QT), 64], F32, ph)
                sinb = sb("sinb", [128, len(QT), 64], F32, ph)
                kng = sb("kng", [128, 128], F32, ph)
                knb = sb("knb", [128, 128], F32, ph)
                pst = [sb("pst%d" % i, [128, 512], F32, ph) for i in range(2)]
                PS = [Buf("pst%d" % i) for i in range(2)]
                rop = [sb("rop%d" % i, [128, 512], F32, ph) for i in range(2)]
                RO = [Buf("rop%d" % i) for i in range(2)]
                tma = sb("tma", [128, 256], F32, ph)
                tmb = sb("tmb", [128, 256], F32, ph)
                TMb = Buf("tm")
                qst = [sb("qst%d" % i, [128, 4, 128], BF16, ph) for i in range(2)]
                QS = [Buf("qst%d" % i) for i in range(2)]
                st6 = sb("st6", [128, 6], F32, ph)
                mv = sb("mv", [128, 2], F32, ph)
                rs = sb("rs", [128, 1], F32, ph)
                iwst = [sb("iwst%d" % i, [128, 32], F32, ph) for i in range(2)]
                IWB = [Buf("iwst%d" % i) for i in range(2)]
                SMb = Buf("small")
                TB = Buf("tabs")
                for ti, (t0, n) in enumerate(QT):
                    dma("sp", out=cosb[0:n, ti, :], in_=cos_tab[t0:t0 + n, :], writes=[TB])
                    dma("sp", out=sinb[0:n, ti, :], in_=sin_tab[t0:t0 + n, :], writes=[TB])
                dma("sp", out=kng[:], in_=kn_g[0, :].partition_broadcast(128), writes=[TB])
                dma("sp", out=knb[:], in_=kn_b[0, :].partition_broadcast(128), writes=[TB])
                groups = [(512 * i, 512, "q", i) for i in range(4)] + [(2048, 512, "k", 0), (2560, 512, "v", 0)] + \
                         [(3072 + 512 * i, 512, "iq", i) for i in range(4)] + [(5120, 144, "ikw", 0)]
                ctr = [0]

                def rope(src, dst, n, ti, nh, sbuf_, dbuf_):
                    x = src.rearrange("p (h t d) -> p h t d", t=2, d=64)
                    o = dst.rearrange("p (h t d) -> p h t d", t=2, d=64)
                    c = bcast_mid(cosb[0:n, ti, :], nh)
                    sn = bcast_mid(sinb[0:n, ti, :], nh)
                    ta = tma[0:n, 0:nh * 64].rearrange("p (h d) -> p h d", d=64)
                    tb = tmb[0:n, 0:nh * 64].rearrange("p (h d) -> p h d", d=64)
                    V = nc.vector
                    op("dve", lambda: V.tensor_tensor(out=ta, in0=x[:, :, 0, :], in1=c, op=ALU.mult), reads=[TB, sbuf_], writes=[TMb])
                    op("dve", lambda: V.tensor_tensor(out=tb, in0=x[:, :, 1, :], in1=sn, op=ALU.mult), reads=[TB, sbuf_], writes=[TMb])
                    op("dve", lambda: V.tensor_tensor(out=o[:, :, 0, :], in0=ta, in1=tb, op=ALU.subtract), reads=[TMb], writes=[dbuf_])
                    op("dve", lambda: V.tensor_tensor(out=ta, in0=x[:, :, 0, :], in1=sn, op=ALU.mult), reads=[TB, sbuf_], writes=[TMb])
                    op("dve", lambda: V.tensor_tensor(out=tb, in0=x[:, :, 1, :], in1=c, op=ALU.mult), reads=[TB, sbuf_], writes=[TMb])
                    op("dve", lambda: V.tensor_tensor(out=o[:, :, 1, :], in0=ta, in1=tb, op=ALU.add), reads=[TMb], writes=[dbuf_])

                def out_rows(ti, t0, n, src_ap, col0, ncols, out_ap, srcbufs, kind):
                    if ti < 8:
                        dma("sp", out=kvi_own[ti].ap()[0:n, col0:col0 + ncols], in_=src_ap, reads=srcbufs,
                            writes=[KVB[kind][ti]])
                    elif ti < 10:
                        dma("sp", out=kvi_samp[t0 - NMAIN:t0 - NMAIN + n, col0:col0 + ncols], in_=src_ap, reads=srcbufs,
                            writes=[KVB[kind][ti]])
                    if ti < 10:
                        dma("sp", out=out_ap[t0:t0 + n, :], in_=src_ap, reads=srcbufs)

                pipe = Pipe(1)
                for gi, (col0, ncols, kind, idx) in enumerate(groups):
                    def load(gi=gi, col0=col0, ncols=ncols):
                        sl = gi % 2
                        dma("pool", out=aw[:, sl, :, 0:ncols],
                            in_=attn_w_in[:, col0:col0 + ncols].rearrange("(k p) c -> p k c", p=128),
                            writes=[AWB[sl]], pool="w")

                    def compute(gi=gi, col0=col0, ncols=ncols, kind=kind, idx=idx):
                        sl = gi % 2
                        pmap = {}

                        def front(ti):
                            t0, n = QT[ti]
                            tt = min(t0 // 512, 2)
                            bi = next_bank()

                            def f():
                                ins = None
                                for kc in range(KC):
                                    ins = nc.tensor.matmul(psum[0:n, bi, 0:ncols], xT[:, kc, t0:t0 + n],
                                                           aw[:, sl, kc, 0:ncols], start=(kc == 0), stop=(kc == KC - 1))
                                return ins
                            op("pe", f, reads=[AWB[sl]] + [XT[kc][tt] for kc in range(KC)], writes=[PB[bi]])
                            p = ctr[0] % 2
                            ctr[0] += 1
                            pmap[ti] = p
                            op("act", lambda: nc.scalar.copy(out=pst[p][0:n, 0:ncols], in_=psum[0:n, bi, 0:ncols]),
                               reads=[PB[bi]], writes=[PS[p]])

                        def back(ti):
                            t0, n = QT[ti]
                            p = pmap[ti]
                            if kind == "v":
                                out_rows(ti, t0, n, pst[p][0:n, 0:512], 512, 512, v_out, [PS[p]], "v")
                            elif kind in ("q", "iq", "k"):
                                rope(pst[p][0:n, 0:512], rop[p][0:n, 0:512], n, ti, 4, PS[p], RO[p])
                                if kind == "k":
                                    out_rows(ti, t0, n, rop[p][0:n, 0:512], 0, 512, k_out, [RO[p]], "k")
                                else:
                                    b2 = next_bank()

                                    def ft():
                                        ins = None
                                        for j in range(4):
                                            ins = nc.tensor.transpose(out=psum[:, b2, j * 128:j * 128 + n],
                                                                      in_=rop[p][0:n, j * 128:(j + 1) * 128],
                                                                      identity=ident[0:n, 0:n])
                                        return ins
                                    op("pe", ft, reads=[RO[p], B_const], writes=[PB[b2]])
                                    op("act", lambda: nc.scalar.copy(
                                        out=qst[p][:, :, 0:n],
                                        in_=psum[:, b2, :].rearrange("p (j c) -> p j c", c=128)[:, :, 0:n]),
                                        reads=[PB[b2]], writes=[QS[p]])
                                    dst = QTs if kind == "q" else IQTs
                                    dma("sp", out=dst[4 * idx:4 * idx + 4, :, t0:t0 + n].rearrange("h d q -> d h q"),
                                        in_=qst[p][:, :, 0:n], reads=[QS[p]], writes=[QTB[kind][ti]])
                            else:
                                V = nc.vector
                                x = pst[p][0:n, 0:128]
                                op("dve", lambda: V.bn_stats(out=st6[0:n, :], in_=x), reads=[PS[p]], writes=[SMb])
                                op("dve", lambda: V.bn_aggr(out=mv[0:n, :], in_=st6[0:n, :]), reads=[SMb], writes=[SMb])
                                op("dve", lambda: V.tensor_scalar(out=rs[0:n, :], in0=mv[0:n, 1:2], scalar1=LN_EPS,
                                                                  scalar2=None, op0=ALU.add), reads=[SMb], writes=[SMb])
                                op("act", lambda: nc.scalar.activation(out=rs[0:n, :], in_=rs[0:n, :], func=AF.Sqrt),
                                   reads=[SMb], writes=[SMb])
                                op("dve", lambda: V.reciprocal(out=rs[0:n, :], in_=rs[0:n, :]), reads=[SMb], writes=[SMb])
                                op("dve", lambda: V.tensor_scalar(out=x, in0=x, scalar1=mv[0:n, 0:1], scalar2=rs[0:n, 0:1],
                                                                  op0=ALU.subtract, op1=ALU.mult),
                                   reads=[SMb, PS[p]], writes=[PS[p]])
                                op("dve", lambda: V.tensor_tensor(out=x, in0=x, in1=kng[0:n, :], op=ALU.mult),
                                   reads=[PS[p], TB], writes=[PS[p]])
                                op("dve", lambda: V.tensor_tensor(out=x, in0=x, in1=knb[0:n, :], op=ALU.add),
                                   reads=[PS[p], TB], writes=[PS[p]])
                                rope(x, rop[p][0:n, 0:128], n, ti, 1, PS[p], RO[p])
                                out_rows(ti, t0, n, rop[p][0:n, 0:128], 1024, 128, ik_out, [RO[p]], "ik")
                                w = pst[p][0:n, 128:144]
                                op("dve", lambda: V.tensor_scalar(out=iwst[p][0:n, 16:32], in0=w, scalar1=0.0,
                                                                  scalar2=2.0, op0=ALU.is_ge, op1=ALU.mult),
                                   reads=[PS[p]], writes=[IWB[p]])
                                op("dve", lambda: V.tensor_scalar(out=iwst[p][0:n, 16:32], in0=iwst[p][0:n, 16:32],
                                                                  scalar1=-1.0, scalar2=None, op0=ALU.add),
                                   reads=[IWB[p]], writes=[IWB[p]])
                                op("dve", lambda: V.scalar_tensor_tensor(
                                    out=iwst[p][0:n, 0:16], in0=w, scalar=IDX_W_SCALE, in1=iwst[p][0:n, 16:32],
                                    op0=ALU.mult, op1=ALU.mult), reads=[PS[p], IWB[p]], writes=[IWB[p]])
                                dma("sp", out=IWs[t0:t0 + n, :], in_=iwst[p][0:n, :], reads=[IWB[p]], writes=[B_IW])
                        for ti in range(len(QT) + 1):
                            if ti < len(QT):
                                front(ti)
                            if ti >= 1:
                                back(ti - 1)
                    pipe.add(load, compute)
                pipe.run()
                kb.barrier()

        def gather_phase():
            if dbg_gather:
                return
            for ti in range(8):
                evs = [KVB[kind][ti].w for kind in ("k", "v", "ik") if KVB[kind][ti].w is not None]
                kb._wait("pool", evs)
                nc.gpsimd.collective_compute("AllGather", ALU.bypass, replica_groups=[[0, 1, 2, 3], [4, 5, 6, 7]],
                                             ins=[kvi_own[ti].ap().opt()],
                                             outs=[kvi_all[ti].ap().opt()]).then_inc(kb.ccsem.h)
                kb.ccsem.val += 1
            B_KVALL.w = (kb.ccsem, kb.ccsem.val)

        def attention_phase():
            def key_src(st):
                if dbg_gather:
                    return kvi_all_in[st * 128:(st + 1) * 128, :]
                r, k = st // 8, st % 8
                return kvi_all[k].ap()[r * 128:(r + 1) * 128, :]

            with contextlib.ExitStack() as ph:
                nbanks[0] = 6
                bO, bL = 6, 7
                LS = PAST + TS
                KCOLS = 2 * LS
                NST_S = (LS + 127) // 128
                KT = sb("KT", [128, NKV, KCOLS], BF16, ph)
                Vt = sb("Vt", [128, 2 * NST_S, 512], BF16, ph)
                ikT = sb("ikT", [128, KCOLS], BF16, ph)
                KEYS_K = Buf("keysk")
                KEYS_V = Buf("keysv")
                SK = [Buf("skeyk0"), Buf("skeyk1")]
                SV = [Buf("skeyv0"), Buf("skeyv1")]
                kst = [sb("kst%d" % i, [128, KVW], F32, ph) for i in range(2)]
                KS = [[Buf("kst%d_%d" % (i, j)) for j in range(3)] for i in range(2)]
                chunk_iota = sb("chunk_iota", [128, SEQ // 64], F32, ph)
                onesb = sb("onesb", [128, 128], BF16, ph)
                NB = 2
                qT = [sb("qT%d" % i, [128, NH, 128], BF16, ph) for i in range(NB)]
                QB = [Buf("q%d" % i) for i in range(NB)]
                iqT = [sb("iqT%d" % i, [128, NH, 128], BF16, ph) for i in range(NB)]
                IQB = [Buf("iq%d" % i) for i in range(NB)]
                iw = [sb("iw%d" % i, [128, 32], F32, ph) for i in range(NB)]
                diag = [sb("diag%d" % i, [128, NH, 128], BF16, ph) for i in range(NB)]
                DGB = [Buf("diag%d" % i) for i in range(NB)]
                identb = sb("identb", [128, 128], BF16, ph)
                kmx = sb("kmx", [128, 1], F32, ph)
                KMB = Buf("kmx")
                I = [sb("I%d" % i, [128, SEQ], F32, ph) for i in range(NB)]
                IB = [Buf("I%d" % i) for i in range(NB)]
                junk = sb("junk", [128, SEQ], BF16, ph)
                JB = Buf("junk")
                M = [sb("M%d" % i, [128, SEQ], BF16, ph) for i in range(NB)]
                MB = [Buf("M%d" % i) for i in range(NB)]
                pen = sb("pen", [128, SEQ // 64], F32, ph)
                NR = 4
                R = [sb("R%d" % i, [128, 512], BF16, ph) for i in range(NR)]
                RB = [Buf("R%d" % i) for i in range(NR)]
                MT = sb("MT", [128, SEQ // 128, 128], BF16, ph)
                MTB = Buf("MT")
                PT = [sb("PT%d" % i, [128, 512], BF16, ph) for i in range(3)]
                PTB = [Buf("PT%d" % i) for i in range(3)]
                aTq = [sb("aTq%d" % i, [128, NH, 128], BF16, ph) for i in range(NB)]
                ATB = [Buf("aTq%d" % i) for i in range(NB)]
                rl = sb("rl", [128, 512], F32, ph)
                RLB = Buf("rl")
                of = sb("of", [128, 512], F32, ph)
                OFB = Buf("of")
                onesf = sb("onesf", [128, 512], F32, ph)
                sc = [sb("sc%d" % i, [128, 8], F32, ph) for i in range(NB)]
                SCB = [Buf("sc%d" % i) for i in range(NB)]
                V = nc.vector
                dma("sp", out=chunk_iota[:], in_=iota_in[:, 0:SEQ // 64], writes=[B_const])
                op("dve", lambda: V.memset(onesb[:], 1.0), writes=[B_const])
                op("dve", lambda: V.tensor_copy(out=identb[:], in_=ident[:]), reads=[B_const], writes=[B_const])
                op("dve", lambda: V.memset(onesf[:], -1.0), writes=[B_const])
                rctr = [0]
                pctr = [0]
                nbanks[0] = 4
                acc_ctr = [0]

                def stage_to_keys(s, nrows, st, col0, vt0, kbuf, vbuf):
                    b = next_bank()

                    def f():
                        ins = None
                        for j in range(4):
                            ins = nc.tensor.transpose(out=psum[:, b, j * 128:j * 128 + nrows],
                                                      in_=kst[s][0:nrows, j * 128:(j + 1) * 128],
                                                      identity=ident[0:nrows, 0:nrows])
                        return ins
                    op("pe", f, reads=KS[s] + [B_const], writes=[PB[b]])
                    c = col0 + st * 128
                    op("act", lambda: nc.scalar.copy(
                        out=KT[:, :, c:c + nrows],
                        in_=psum[:, b, :].rearrange("p (j c) -> p j c", c=128)[:, :, 0:nrows]),
                        reads=[PB[b]], writes=kbuf)
                    b2 = next_bank()
                    op("pe", lambda: nc.tensor.transpose(out=psum[:, b2, 0:nrows], in_=kst[s][0:nrows, 1024:1152],
                                                         identity=ident[0:nrows, 0:nrows]),
                       reads=KS[s] + [B_const], writes=[PB[b2]])
                    op("act", lambda: nc.scalar.copy(out=ikT[:, c:c + nrows], in_=psum[:, b2, 0:nrows]),
                       reads=[PB[b2]], writes=kbuf)
                    op("dve", lambda: V.tensor_copy(out=Vt[0:nrows, vt0 + st, :], in_=kst[s][0:nrows, 512:1024]),
                       reads=KS[s], writes=vbuf)

                class QTile:
                    pass

                def make_tile(i, tok0, nq, L, masked, col0, vt0, kbuf, vbuf):
                    t = QTile()
                    t.p = i % NB
                    t.tok0, t.nq, t.L, t.masked = tok0, nq, L, masked
                    t.col0, t.vt0, t.kbuf, t.vbuf = col0, vt0, kbuf, vbuf
                    t.nst = (L + 127) // 128
                    return t

                def index_stage(t):
                    p, nq, L, tok0 = t.p, t.nq, t.L, t.tok0
                    dma("sp", out=iqT[p][:, :, 0:nq], in_=IQTs[:, :, tok0:tok0 + nq].rearrange("h d q -> d h q"),
                        writes=[IQB[p]])
                    dma("sp", out=iw[p][0:nq, :], in_=IWs[tok0:tok0 + nq, :], writes=[IQB[p]])
                    for h in range(NH):
                        op("dve", lambda: V.tensor_scalar(out=diag[p][0:nq, h, 0:nq], in0=identb[0:nq, 0:nq],
                                                          scalar1=iw[p][0:nq, 16 + h:17 + h], scalar2=None,
                                                          op0=ALU.mult),
                           reads=[IQB[p], B_const], writes=[DGB[p]])
                    steps = []
                    for c0 in range(0, L, 512):
                        cn = min(512, L - c0)
                        ba = 4 + acc_ctr[0] % 2
                        acc_ctr[0] += 1
                        for h in range(NH):
                            steps.append((c0, cn, ba, h))
                    rmap = {}

                    def front(j):
                        c0, cn, ba, h = steps[j]
                        b = next_bank()
                        op("pe", lambda: nc.tensor.matmul(psum[0:nq, b, 0:cn], iqT[p][:, h, 0:nq],
                                                          ikT[:, t.col0 + c0:t.col0 + c0 + cn],
                                                          start=True, stop=True),
                           reads=[IQB[p]] + t.kbuf, writes=[PB[b]])
                        r = rctr[0] % NR
                        rctr[0] += 1
                        rmap[j] = r
                        op("act", lambda: nc.scalar.activation(out=R[r][0:nq, 0:cn], in_=psum[0:nq, b, 0:cn],
                                                               func=AF.Relu, scale=iw[p][0:nq, h:h + 1]),
                           reads=[PB[b], IQB[p]], writes=[RB[r]])

                    def back(j):
                        c0, cn, ba, h = steps[j]
                        r = rmap[j]
                        op("pe", lambda: nc.tensor.matmul(psum[0:nq, ba, 0:cn], diag[p][0:nq, h, 0:nq],
                                                          R[r][0:nq, 0:cn], start=(h == 0), stop=(h == NH - 1)),
                           reads=[DGB[p], RB[r]], writes=[PB[ba]])
                        if h == NH - 1:
                            op("act", lambda: nc.scalar.copy(out=I[p][0:nq, c0:c0 + cn], in_=psum[0:nq, ba, 0:cn]),
                               reads=[PB[ba]], writes=[IB[p]])
                    SKW = 2
                    for j in range(len(steps) + SKW):
                        if j < len(steps):
                            front(j)
                        if j >= SKW:
                            back(j - SKW)

                def bisect_stage(t):
                    p, nq, L = t.p, t.nq, t.L
                    Iv = I[p][0:nq, 0:L]
                    lo, w0, tt_, cnt, gek, hi = (sc[p][0:nq, i:i + 1] for i in range(6))
                    op("dve", lambda: V.tensor_reduce(out=lo, in_=Iv, axis=AX.X, op=ALU.min), reads=[IB[p]], writes=[SCB[p]])
                    op("dve", lambda: V.tensor_reduce(out=hi, in_=Iv, axis=AX.X, op=ALU.max), reads=[IB[p]], writes=[SCB[p]])
                    op("dve", lambda: V.scalar_tensor_tensor(out=w0, in0=hi, scalar=1.0, in1=lo, op0=ALU.add,
                                                             op1=ALU.subtract), reads=[SCB[p]], writes=[SCB[p]])
                    if t.masked:
                        dma("sp", out=kmx[0:nq, :], in_=kmax_tab[t.tok0:t.tok0 + nq, :], writes=[KMB])
                        op("dve", lambda: V.tensor_scalar(out=pen[0:nq, :], in0=chunk_iota[0:nq, :],
                                                          scalar1=kmx[0:nq, 0:1], scalar2=NEG,
                                                          op0=ALU.is_ge, op1=ALU.mult),
                           reads=[B_const, KMB], writes=[JB])
                        Iv3 = Iv.rearrange("p (c k) -> p c k", k=64)
                        pa = pen[0:nq, 0:L // 64]
                        pl = [list(x) for x in pa.ap]
                        pen_b = bass.AP(pa.tensor, pa.offset, [pl[0], pl[1], [0, 64]])
                        op("dve", lambda: V.tensor_tensor(out=Iv3, in0=Iv3, in1=pen_b, op=ALU.add),
                           reads=[IB[p], JB], writes=[IB[p]])
                    for it in range(NBIS):
                        wk = 0.5 ** (it + 1)
                        op("dve", lambda: V.scalar_tensor_tensor(out=tt_, in0=w0, scalar=wk, in1=lo, op0=ALU.mult,
                                                                 op1=ALU.add), reads=[SCB[p]], writes=[SCB[p]])
                        op("dve", lambda: V.tensor_scalar(out=junk[0:nq, 0:L], in0=Iv, scalar1=tt_, scalar2=0.0,
                                                          op0=ALU.is_ge, op1=ALU.add, accum_out=cnt),
                           reads=[IB[p], SCB[p]], writes=[JB, SCB[p]])
                        op("dve", lambda: V.tensor_scalar(out=gek, in0=cnt, scalar1=float(TOPK), scalar2=wk,
                                                          op0=ALU.is_ge, op1=ALU.mult), reads=[SCB[p]], writes=[SCB[p]])
                        op("dve", lambda: V.scalar_tensor_tensor(out=lo, in0=w0, scalar=gek, in1=lo, op0=ALU.mult,
                                                                 op1=ALU.add), reads=[SCB[p]], writes=[SCB[p]])
                    op("dve", lambda: V.tensor_scalar(out=M[p][0:nq, 0:L], in0=Iv, scalar1=lo, scalar2=None,
                                                      op0=ALU.is_ge), reads=[IB[p], SCB[p]], writes=[MB[p]])

                def attend_stage(t):
                    p, nq, L, tok0 = t.p, t.nq, t.L, t.tok0
                    dma("sp", out=qT[p][:, :, 0:nq], in_=QTs[:, :, tok0:tok0 + nq].rearrange("h d q -> d h q"),
                        writes=[QB[p]])
                    for s0 in range(0, t.nst, 8):
                        b = next_bank()
                        sts = list(range(s0, min(s0 + 8, t.nst)))
                        pb16 = psum[:, b, :].bitcast(BF16)

                        def f():
                            ins = None
                            for j, st in enumerate(sts):
                                ns = min(128, L - st * 128)
                                ins = nc.tensor.transpose(out=pb16[0:ns, j * 128:j * 128 + nq],
                                                          in_=M[p][0:nq, st * 128:st * 128 + ns],
                                                          identity=identb[0:nq, 0:nq])
                            return ins
                        op("pe", f, reads=[MB[p], B_const], writes=[PB[b]])
                        full = [st for st in sts if min(128, L - st * 128) == 128]
                        if full:
                            op("act", lambda: nc.scalar.copy(
                                out=MT[:, full[0]:full[0] + len(full), 0:nq],
                                in_=pb16[:, 0:len(full) * 128].rearrange("p (j c) -> p j c", c=128)[:, :, 0:nq]),
                                reads=[PB[b]], writes=[MTB])
                        for j, st in enumerate(sts):
                            ns = min(128, L - st * 128)
                            if ns < 128:
                                op("act", lambda: nc.scalar.copy(out=MT[0:ns, st, 0:nq],
                                                                 in_=pb16[0:ns, j * 128:j * 128 + nq]),
                                   reads=[PB[b]], writes=[MTB])
                    steps = [(g, st) for g in range(NKV) for st in range(t.nst)]
                    kmap = {}

                    def front(j):
                        g, st = steps[j]
                        ns = min(128, L - st * 128)
                        b = next_bank()
                        c = t.col0 + st * 128
                        op("pe", lambda: nc.tensor.matmul(
                            psum[0:ns, b, 0:4 * nq].rearrange("p (h q) -> p h q", q=nq),
                            KT[:, g, c:c + ns], qT[p][:, 4 * g:4 * g + 4, 0:nq], start=True, stop=True),
                            reads=t.kbuf + [QB[p]], writes=[PB[b]])
                        k = pctr[0] % 3
                        pctr[0] += 1
                        kmap[j] = k
                        op("act", lambda: nc.scalar.activation(out=PT[k][0:ns, 0:4 * nq], in_=psum[0:ns, b, 0:4 * nq],
                                                               func=AF.Exp, scale=ATTN_SCALE),
                           reads=[PB[b]], writes=[PTB[k]])
                        op("dve", lambda: V.tensor_tensor(
                            out=PT[k][0:ns, 0:4 * nq].rearrange("p (h q) -> p h q", q=nq),
                            in0=PT[k][0:ns, 0:4 * nq].rearrange("p (h q) -> p h q", q=nq),
                            in1=bcast_mid(MT[0:ns, st, 0:nq], 4), op=ALU.mult),
                            reads=[PTB[k], MTB], writes=[PTB[k]])

                    def back(j):
                        g, st = steps[j]
                        ns = min(128, L - st * 128)
                        k = kmap[j]
                        op("pe", lambda: nc.tensor.matmul(psum[:, bO, 0:4 * nq],
                                                          Vt[0:ns, t.vt0 + st, g * 128:(g + 1) * 128],
                                                          PT[k][0:ns, 0:4 * nq], start=(st == 0),
                                                          stop=(st == t.nst - 1)),
                           reads=t.vbuf + [PTB[k]], writes=[PB[bO]])
                        op("pe", lambda: nc.tensor.matmul(psum[:, bL, 0:4 * nq], onesb[0:ns, :],
                                                          PT[k][0:ns, 0:4 * nq], start=(st == 0),
                                                          stop=(st == t.nst - 1)),
                           reads=[B_const, PTB[k]], writes=[PB[bL]])
                        if st == t.nst - 1:
                            op("dve", lambda: V.reciprocal(out=rl[:, 0:4 * nq], in_=psum[:, bL, 0:4 * nq]),
                               reads=[PB[bL]], writes=[RLB])
                            op("dve", lambda: V.tensor_tensor(
                                out=aTq[p][:, 4 * g:4 * g + 4, 0:nq],
                                in0=psum[:, bO, 0:4 * nq].rearrange("p (h q) -> p h q", q=nq),
                                in1=rl[:, 0:4 * nq].rearrange("p (h q) -> p h q", q=nq), op=ALU.mult),
                                reads=[PB[bO], RLB], writes=[ATB[p]])
                    SKW = 2
                    for j in range(len(steps) + SKW):
                        if j < len(steps):
                            front(j)
                        if j >= SKW:
                            back(j - SKW)
                    dma("sp", out=ATs[:, :, tok0:tok0 + nq].rearrange("h d q -> d h q"), in_=aTq[p][:, :, 0:nq],
                        reads=[ATB[p]], writes=[B_AT])

                def load_prompt_keys():
                    for st in range(SEQ // 128):
                        s = st % 2
                        dma("sp", out=kst[s][:, :], in_=key_src(st), reads=[B_KVALL], writes=KS[s])
                        stage_to_keys(s, 128, st, 0, 0, [KEYS_K], [KEYS_V])

                def load_sample_keys(i):
                    j = i % 2
                    kbuf, vbuf = [SK[j], KEYS_K], [SV[j], KEYS_V]
                    for st in range(PAST // 128):
                        s = st % 2
                        dma("sp", out=kst[s][:, 0:512], in_=cache_k[i, st * 128:(st + 1) * 128, :], writes=[KS[s][0]])
                        dma("sp", out=kst[s][:, 512:1024], in_=cache_v[i, st * 128:(st + 1) * 128, :], writes=[KS[s][1]])
                        dma("sp", out=kst[s][:, 1024:1152], in_=cache_ik[i, st * 128:(st + 1) * 128, :], writes=[KS[s][2]])
                        stage_to_keys(s, 128, st, j * LS, j * NST_S, kbuf, vbuf)
                    st = PAST // 128
                    s = st % 2
                    dma("sp", out=kst[s][0:TS, :], in_=kvi_samp[i * TS:(i + 1) * TS, :], writes=KS[s])
                    stage_to_keys(s, TS, st, j * LS, j * NST_S, kbuf, vbuf)

                tiles = []
                for i, (t0, n) in enumerate(QT[:8] + [QT[10]]):
                    Lt = 3 * NMAIN + 128 * (i + 1) if i < 8 else 3 * NMAIN
                    tiles.append(make_tile(i, t0, n, Lt, True, 0, 0, [KEYS_K], [KEYS_V]))
                for i in range(NSEQ):
                    j = i % 2
                    tiles.append(make_tile(9 + i, NMAIN + i * TS, TS, LS, False, j * LS, j * NST_S, [SK[j]], [SV[j]]))
                load_prompt_keys()
                n = len(tiles)

                def emit_index(k):
                    if k >= 9:
                        load_sample_keys(k - 9)
                    index_stage(tiles[k])
                emit_index(0)
                for i in range(n + 1):
                    if i >= 1:
                        attend_stage(tiles[i - 1])
                    if i == 9:
                        emit_index(9)
                    if i + 1 < n and i + 1 != 9:
                        emit_index(i + 1)
                    if i < n:
                        bisect_stage(tiles[i])
                nbanks[0] = 8
                kb.barrier()

        def spill_xres():
            for kc in range(KC):
                dma("sp", out=XSP[:, kc, :], in_=xres[:, kc, :], reads=[XR[kc][t] for t in range(3)])
            kb.barrier()

        def reload_xres():
            for kc in range(KC):
                dma("sp", out=xres[:, kc, :], in_=XSP[:, kc, :], writes=[XR[kc][t] for t in range(3)])

        def attn_out_phase():
            with contextlib.ExitStack() as ph:
                aT = sb("aT", [128, KC, T], BF16, ph)
                AB = [Buf("aT%d" % h) for h in range(NH)]
                for h in range(NH):
                    dma("sp", out=aT[:, h, :], in_=ATs[h, :, :], writes=[AB[h]])
                pipe = Pipe(3)
                for m in range(KC):
                    slots = []

                    def load(m=m, slots=slots):
                        sl = next_w()
                        slots.append(sl)
                        load_w(sl, attn_w_out[:, m * 128:(m + 1) * 128], KC)

                    def compute(m=m, slots=slots):
                        sl = slots[0]
                        for tt, (lo, hi) in enumerate(TT):
                            n = hi - lo
                            bi = next_bank()

                            def f():
                                ins = None
                                for kc in range(KC):
                                    ins = nc.tensor.matmul(psum[:, bi, 0:n], wring[:, sl, kc, :], aT[:, kc, lo:hi],
                                                           start=(kc == 0), stop=(kc == KC - 1))
                                return ins
                            op("pe", f, reads=[WB[sl]] + AB, writes=[PB[bi]])
                            residual_add(True, m, tt, bi)
                    pipe.add(load, compute)
                pipe.run()
                kb.barrier()
            ln_phase(2)

        def output_phase():
            with contextlib.ExitStack() as ph:
                ostg = [sb("ostg%d" % i, [128, D], F32, ph) for i in range(2)]
                OS = [Buf("ostg%d" % i) for i in range(2)]
                for qi, (t0, n) in enumerate(QT[:10]):
                    s = qi % 2
                    tt = min(t0 // 512, 2)
                    for g4 in range(4):
                        bi = next_bank()

                        def f(g4=g4, bi=bi, t0=t0, n=n):
                            ins = None
                            for j in range(4):
                                kc = g4 * 4 + j
                                ins = nc.tensor.transpose(out=psum[0:n, bi, j * 128:(j + 1) * 128],
                                                          in_=xres[:, kc, t0:t0 + n], identity=ident[:, :])
                            return ins
                        op("pe", f, reads=[XR[g4 * 4 + j][tt] for j in range(4)] + [B_const], writes=[PB[bi]])
                        eng = "act" if g4 % 2 == 0 else "dve"
                        if eng == "act":
                            op("act", lambda g4=g4, bi=bi, s=s, n=n: nc.scalar.copy(
                                out=ostg[s][0:n, g4 * 512:(g4 + 1) * 512], in_=psum[0:n, bi, :]),
                                reads=[PB[bi]], writes=[OS[s]])
                        else:
                            op("dve", lambda g4=g4, bi=bi, s=s, n=n: nc.vector.tensor_copy(
                                out=ostg[s][0:n, g4 * 512:(g4 + 1) * 512], in_=psum[0:n, bi, :]),
                                reads=[PB[bi]], writes=[OS[s]])
                    dma("sp", out=y_out[t0:t0 + n, :], in_=ostg[s][0:n, :], reads=[OS[s]])
                kb.barrier()

        if stop_after not in ("p0",):
            mixer_phase()
        if stop_after not in ("p0", "mix"):
            ffn_phase(0)
        if full:
            attn_proj_phase()
            gather_phase()
            spill_xres()
            act.close()
            attention_phase()
            act = contextlib.ExitStack()
            xres = sb("xres_b", [128, KC, T], F32, act)
            xT = sb("xT_b", [128, KC, T], BF16, act)
            wring = sb("wring_b", [128, NW, KC, 128], BF16, act)
            reload_xres()
            attn_out_phase()
            ffn_phase(1)
        output_phase()
        kb.barrier()
        act.close()
        kb.barrier()
    return nc


_IDENT = np.eye(128, dtype=np.float32)
_IOTA = np.ascontiguousarray(np.broadcast_to(np.arange(SEQ, dtype=np.float32)[None, :], (128, SEQ)))


def _pos_tables(q):
    t0 = q * NMAIN
    pos = np.concatenate([t0 + np.arange(NMAIN), np.tile(PAST + np.arange(TS), NSEQ),
                          np.maximum(t0 - HALO + np.arange(HALO), 0)]).astype(np.int64)
    inv_freq = (np.float32(10000.0) ** (-np.arange(64, dtype=np.float32) / np.float32(64))).astype(np.float32)
    ang = pos.astype(np.float32)[:, None] * inv_freq[None, :]
    kmax = (pos // 64 + 1).astype(np.float32)[:, None]
    return np.cos(ang).astype(np.float32), np.sin(ang).astype(np.float32), np.ascontiguousarray(kmax)


def make_in_maps(inputs, full=True):
    g = lambda k: np.asarray(inputs[k])
    xp = g("x_prompt")
    xs = g("x_sample")
    in_maps = []
    for c in range(NCORES):
        b, q = c // 4, c % 4
        t0 = q * NMAIN
        main = xp[b, t0:t0 + NMAIN]
        if q == 0:
            halo = np.zeros((HALO, D), np.float32)
        else:
            halo = xp[b, t0 - HALO:t0]
        samp = xs[c * NSEQ:(c + 1) * NSEQ].reshape(NSEQ * TS, D)
        x_tok = np.ascontiguousarray(np.concatenate([main, samp, halo], axis=0))
        sl = slice(c * NSEQ, (c + 1) * NSEQ)
        m = {
            "x_tok": x_tok,
            "hm": np.full((128, 1), 0.0 if q == 0 else 1.0, np.float32),
            "ident": _IDENT,
            "st_mix": np.ascontiguousarray(g("state_conv_mix")[0, sl].reshape(NSEQ * 2, D)),
            "st_ffn": np.ascontiguousarray(g("state_ffn_conv")[:, sl].reshape(2, NSEQ * 2, DFF)),
            "mix_w_in": g("mix_w_in")[0],
            "mix_conv_w": g("mix_conv_w")[0],
            "mix_w_out": g("mix_w_out")[0],
            "ffn_w_in": g("ffn_w_in"),
            "ffn_conv_w": g("ffn_conv_w"),
            "ffn_conv_b": g("ffn_conv_b"),
            "ffn_w_down": g("ffn_w_down"),
            "ln_g": np.ascontiguousarray(np.stack([g("ln1_g")[0], g("ln2_g")[0], g("ln1_g")[1], g("ln2_g")[1]])),
            "ln_b": np.ascontiguousarray(np.stack([g("ln1_b")[0], g("ln2_b")[0], g("ln1_b")[1], g("ln2_b")[1]])),
        }
        if full:
            cos, sin, kmax = _pos_tables(q)
            m.update({
                "attn_w_in": g("attn_w_in")[0],
                "attn_w_out": g("attn_w_out")[0],
                "kn_g": g("idx_k_norm_g"),
                "kn_b": g("idx_k_norm_b"),
                "cos_tab": cos, "sin_tab": sin, "kmax_tab": kmax, "iota": _IOTA,
                "cache_k": np.ascontiguousarray(g("cache_k")[0, sl].reshape(NSEQ, PAST, 512)),
                "cache_v": np.ascontiguousarray(g("cache_v")[0, sl].reshape(NSEQ, PAST, 512)),
                "cache_ik": np.ascontiguousarray(g("cache_idx_k")[0, sl]),
            })
        in_maps.append(m)
    return in_maps


def assemble(results):
    BATCH, DEC = 2, 32
    y_p = np.zeros((BATCH, SEQ, D), np.float32)
    y_s = np.zeros((DEC, TS, D), np.float32)
    cm_p = np.zeros((1, BATCH, 2, D), np.float32)
    k_p = np.zeros((1, BATCH, SEQ, NKV, HD), np.float32)
    v_p = np.zeros((1, BATCH, SEQ, NKV, HD), np.float32)
    ik_p = np.zeros((1, BATCH, SEQ, 128), np.float32)
    ff_p = np.zeros((2, BATCH, 2, DFF), np.float32)
    cm_s = np.zeros((1, DEC, 2, D), np.float32)
    k_s = np.zeros((1, DEC, TS, NKV, HD), np.float32)
    v_s = np.zeros((1, DEC, TS, NKV, HD), np.float32)
    ik_s = np.zeros((1, DEC, TS, 128), np.float32)
    ff_s = np.zeros((2, DEC, 2, DFF), np.float32)
    for c in range(NCORES):
        r = results[c]
        b, q = c // 4, c % 4
        t0 = q * NMAIN
        sl = slice(c * NSEQ, (c + 1) * NSEQ)
        y = r["y_out"]
        y_p[b, t0:t0 + NMAIN] = y[:NMAIN]
        y_s[sl] = y[NMAIN:].reshape(NSEQ, TS, D)
        mt = r["mixtail_out"].reshape((1 + NSEQ) * 2, D)
        ft = r["ffntail_out"].reshape(2, (1 + NSEQ) * 2, DFF)
        cm_s[0, sl] = mt[2:].reshape(NSEQ, 2, D)
        ff_s[:, sl] = ft[:, 2:].reshape(2, NSEQ, 2, DFF)
        if q == 3:
            cm_p[0, b] = mt[:2]
            ff_p[:, b] = ft[:, :2]
        k_p[0, b, t0:t0 + NMAIN] = r["k_out"][:NMAIN].reshape(NMAIN, NKV, HD)
        v_p[0, b, t0:t0 + NMAIN] = r["v_out"][:NMAIN].reshape(NMAIN, NKV, HD)
        ik_p[0, b, t0:t0 + NMAIN] = r["ik_out"][:NMAIN]
        k_s[0, sl] = r["k_out"][NMAIN:].reshape(NSEQ, TS, NKV, HD)
        v_s[0, sl] = r["v_out"][NMAIN:].reshape(NSEQ, TS, NKV, HD)
        ik_s[0, sl] = r["ik_out"][NMAIN:].reshape(NSEQ, TS, 128)
    return (y_p, y_s, cm_p, k_p, v_p, ik_p, ff_p, cm_s, k_s, v_s, ik_s, ff_s)


def kernel(**inputs):
    nc = build_program()
    res = run_bass_kernel_spmd(nc, make_in_maps(inputs), core_ids=list(range(NCORES)))
    return assemble(res.results)
```
